# Optimizing a Trainium2 kernel written in Bass

```python
import math
import jax, jax.numpy as jnp
from jax import lax
import numpy as np

D_MODEL = 2048
BATCH = 4
SEQ = 4096
DEPTH = 4

HEAD_DIM = 64
N_HEADS_A = D_MODEL // HEAD_DIM
N_KV_A = N_HEADS_A // 8
GROUP_A = N_HEADS_A // N_KV_A
N_HEADS_B = D_MODEL // HEAD_DIM
WINDOW = 128
BLOCK = 128
D_FF = 4 * D_MODEL
N_MIXERS = 2
N_A_LAYERS = (DEPTH + 1) // 2
N_B_LAYERS = DEPTH // 2
RMS_EPS = 1e-5

kernel_name = "hybrid_swa_sink_alibi_stickbreaking_sqrelu"


def rmsnorm(x, gain):
    xf = x.astype(jnp.float32)
    y = xf * lax.rsqrt(jnp.mean(xf * xf, axis=-1, keepdims=True) + RMS_EPS)
    return (y * gain.astype(jnp.float32)).astype(x.dtype)


def alibi_slopes(n_heads):
    return jnp.power(2.0, -8.0 * (jnp.arange(n_heads, dtype=jnp.float32) + 1.0) / n_heads)


def sliding_window_sink_attention(xn, w_qkv, w_o, sinks):
    b, s, _ = xn.shape
    nb = s // BLOCK
    qkv = xn @ w_qkv
    q_w, k_w = N_HEADS_A * HEAD_DIM, N_KV_A * HEAD_DIM
    q = qkv[..., :q_w].reshape(b, nb, BLOCK, N_KV_A, GROUP_A, HEAD_DIM)
    k = qkv[..., q_w:q_w + k_w].reshape(b, nb, BLOCK, N_KV_A, HEAD_DIM)
    v = qkv[..., q_w + k_w:].reshape(b, nb, BLOCK, N_KV_A, HEAD_DIM)
    pad = ((0, 0), (1, 0), (0, 0), (0, 0), (0, 0))
    k_win = jnp.concatenate([jnp.pad(k, pad)[:, :-1], k], axis=2)
    v_win = jnp.concatenate([jnp.pad(v, pad)[:, :-1], v], axis=2)
    scale = 1.0 / math.sqrt(HEAD_DIM)
    scores = jnp.einsum('bnqhgd,bnkhd->bnhgqk', q, k_win).astype(jnp.float32) * scale
    dist = (jnp.arange(BLOCK)[:, None] + BLOCK) - jnp.arange(2 * BLOCK)[None, :]
    key_abs = (jnp.arange(nb)[:, None] - 1) * BLOCK + jnp.arange(2 * BLOCK)[None, :]
    valid = ((dist >= 0) & (dist < WINDOW))[None, :, :] & (key_abs >= 0)[:, None, :]
    slopes = alibi_slopes(N_HEADS_A).reshape(N_KV_A, GROUP_A)
    scores = scores - slopes[None, None, :, :, None, None] * dist.astype(jnp.float32)
    scores = jnp.where(valid[None, :, None, None], scores, -jnp.inf)
    sink = sinks.astype(jnp.float32).reshape(N_KV_A, GROUP_A)[None, None, :, :, None, None]
    m = jnp.maximum(jnp.max(scores, axis=-1, keepdims=True), sink)
    p = jnp.exp(scores - m)
    p = p / (jnp.sum(p, axis=-1, keepdims=True) + jnp.exp(sink - m))
    out = jnp.einsum('bnhgqk,bnkhd->bnqhgd', p.astype(v.dtype), v_win)
    return out.reshape(b, s, N_HEADS_A * HEAD_DIM) @ w_o


def stick_breaking_attention(xn, w_qkv, w_o):
    b, s, _ = xn.shape
    nb = s // BLOCK
    qkv = xn @ w_qkv
    w = N_HEADS_B * HEAD_DIM
    q = qkv[..., :w].reshape(b, nb, BLOCK, N_HEADS_B, HEAD_DIM)
    k = qkv[..., w:2 * w].reshape(b, s, N_HEADS_B, HEAD_DIM)
    v = qkv[..., 2 * w:].reshape(b, s, N_HEADS_B, HEAD_DIM)
    q_blocks = jnp.moveaxis(q, 1, 0)
    starts = jnp.arange(nb, dtype=jnp.int32) * BLOCK
    scale = 1.0 / math.sqrt(HEAD_DIM)
    key_pos = jnp.arange(s, dtype=jnp.int32)

    def one_block(args):
        qb, start = args
        z = jnp.einsum('bqhd,bkhd->bhqk', qb, k).astype(jnp.float32) * scale
        t = start + jnp.arange(BLOCK, dtype=jnp.int32)
        before = (key_pos[None, :] < t[:, None])[None, None]
        log_beta = jax.nn.log_sigmoid(z)
        log_1m_beta = jnp.where(before, jax.nn.log_sigmoid(-z), 0.0)
        suffix = lax.cumsum(log_1m_beta, axis=3, reverse=True) - log_1m_beta
        a = jnp.where(before, jnp.exp(log_beta + suffix), 0.0)
        return jnp.einsum('bhqk,bkhd->bqhd', a.astype(v.dtype), v)

    out = lax.map(one_block, (q_blocks, starts))
    out = jnp.moveaxis(out, 0, 1).reshape(b, s, N_HEADS_B * HEAD_DIM)
    return out @ w_o


def squared_relu_mlp(xn, w_in, w_out):
    h = jax.nn.relu(xn @ w_in)
    return (h * h) @ w_out


def setup_inputs(seed: int = 0) -> dict:
    key = jax.random.key(seed)
    ks = jax.random.split(key, 12)
    d = D_MODEL
    qkv_a = (N_HEADS_A + 2 * N_KV_A) * HEAD_DIM
    qkv_b = 3 * N_HEADS_B * HEAD_DIM
    x = jax.random.normal(ks[0], (BATCH, SEQ, d), jnp.float32)
    a_w_qkv = jax.random.normal(ks[1], (N_A_LAYERS, d, qkv_a), jnp.float32) * d ** -0.5
    a_w_o = jax.random.normal(ks[2], (N_A_LAYERS, N_HEADS_A * HEAD_DIM, d), jnp.float32) * (N_HEADS_A * HEAD_DIM) ** -0.5
    a_sinks = jax.random.normal(ks[3], (N_A_LAYERS, N_HEADS_A), jnp.float32) * 0.5
    b_w_qkv = jax.random.normal(ks[4], (N_B_LAYERS, d, qkv_b), jnp.float32) * d ** -0.5
    b_w_o = jax.random.normal(ks[5], (N_B_LAYERS, N_HEADS_B * HEAD_DIM, d), jnp.float32) * (N_HEADS_B * HEAD_DIM) ** -0.5
    norm_mix = 1.0 + 0.02 * jax.random.normal(ks[6], (DEPTH, d), jnp.float32)
    norm_mlp = 1.0 + 0.02 * jax.random.normal(ks[7], (DEPTH, d), jnp.float32)
    mlp_w_in = jax.random.normal(ks[8], (DEPTH, d, D_FF), jnp.float32) * d ** -0.5
    mlp_w_out = jax.random.normal(ks[9], (DEPTH, D_FF, d), jnp.float32) * D_FF ** -0.5
    final_norm = 1.0 + 0.02 * jax.random.normal(ks[10], (d,), jnp.float32)
    return {"x": x, "a_w_qkv": a_w_qkv, "a_w_o": a_w_o, "a_sinks": a_sinks,
            "b_w_qkv": b_w_qkv, "b_w_o": b_w_o, "norm_mix": norm_mix,
            "norm_mlp": norm_mlp, "mlp_w_in": mlp_w_in, "mlp_w_out": mlp_w_out,
            "final_norm": final_norm}


def reference(x, a_w_qkv, a_w_o, a_sinks, b_w_qkv, b_w_o, norm_mix, norm_mlp,
              mlp_w_in, mlp_w_out, final_norm):
    for i in range(DEPTH):
        h = rmsnorm(x, norm_mix[i])
        j = i // N_MIXERS
        if i % N_MIXERS == 0:
            x = x + sliding_window_sink_attention(h, a_w_qkv[j], a_w_o[j], a_sinks[j])
        else:
            x = x + stick_breaking_attention(h, b_w_qkv[j], b_w_o[j])
        h = rmsnorm(x, norm_mlp[i])
        x = x + squared_relu_mlp(h, mlp_w_in[i], mlp_w_out[i])
    return rmsnorm(x, final_norm)
```

```python
import math
from contextlib import ExitStack

import numpy as np
import ml_dtypes

import concourse.bass as bass
import concourse.mybir as mybir
from concourse.bass_utils import run_bass_kernel_spmd

F32 = mybir.dt.float32
BF16 = mybir.dt.bfloat16
AF = mybir.ActivationFunctionType
ALU = mybir.AluOpType
AX = mybir.AxisListType

HD = 64
BLK = 128
EPS = 1e-5
NEG = -30000.0
SEM_ROT = 16000


class Cfg:
    def __init__(self, D=2048, S=4096, DEPTH=4, T=1024):
        self.D, self.S, self.DEPTH, self.T = D, S, DEPTH, T
        self.KC = D // 128
        self.NH = D // HD
        self.NKV = self.NH // 8
        self.DKV = self.NKV * HD
        self.DFF = 4 * D
        self.NB = S // BLK
        self.NT = S // T
        self.TH = T // 512
        self.NA = (DEPTH + 1) // 2
        self.NBL = DEPTH // 2
        self.QKVA = D + 2 * self.DKV
        self.QKVB = 3 * D
        self.WS = 4 * D


class Ctx:
    def __init__(self, nc):
        self.nc = nc
        self.eng = {"pe": nc.tensor, "act": nc.scalar, "dve": nc.vector,
                    "pool": nc.gpsimd, "sp": nc.sync}
        self.sems = []
        self.cur = {}
        self.pe_sems = set()
        self.waited = {e: {} for e in self.eng}
        self.last_w = {}
        self.readers = {}
        self.dma = {}
        self.nops = 0

    def _newsem(self):
        self.sems.append(self.nc.alloc_semaphore(f"s{len(self.sems)}"))
        return len(self.sems) - 1

    def _tok(self, e):
        c = self.cur.get(e)
        if c is None or c[1] >= SEM_ROT:
            c = [self._newsem(), 0]
            self.cur[e] = c
            if e == "pe":
                self.pe_sems.add(c[0])
        c[1] += 1
        return (c[0], c[1])

    def op(self, e, fn, r=(), w=(), dma=None):
        deps = {}

        def add(tok):
            if tok is not None and deps.get(tok[0], 0) < tok[1]:
                deps[tok[0]] = tok[1]

        for k in r:
            add(self.last_w.get(k))
        for k in w:
            add(self.last_w.get(k))
            for tok in self.readers.get(k, {}).values():
                add(tok)
        d = None
        if dma is not None:
            d = self.dma.get(dma)
            if d is None:
                d = self.dma[dma] = [self._newsem(), 0]
            if d[1] > 0:
                add((d[0], d[1]))
        E = self.eng[e]
        for s, v in deps.items():
            if e == "pe" and s in self.pe_sems:
                continue
            if self.waited[e].get(s, 0) >= v:
                continue
            E.wait_ge(self.sems[s], v)
            self.waited[e][s] = v
        insts = fn(E)
        if not isinstance(insts, (list, tuple)):
            insts = [insts]
        self.nops += len(insts)
        if dma is not None:
            for i in insts:
                i.then_inc(self.sems[d[0]], 16)
                d[1] += 16
            tok = (d[0], d[1])
            rkey = ("dma", dma)
        else:
            tok = self._tok(e)
            insts[-1].then_inc(self.sems[tok[0]], 1)
            rkey = e
        for k in r:
            self.readers.setdefault(k, {})[rkey] = tok
        for k in w:
            self.last_w[k] = tok
            self.readers[k] = {}
        return tok

    def all_tokens(self):
        toks = [(c[0], c[1]) for c in self.cur.values()]
        toks += [(d[0], d[1]) for d in self.dma.values() if d[1] > 0]
        return toks

    def barrier(self, engines=None):
        toks = self.all_tokens()
        for e, E in self.eng.items():
            if engines is not None and e not in engines:
                continue
            for s, v in toks:
                if self.waited[e].get(s, 0) < v:
                    E.wait_ge(self.sems[s], v)
                    self.waited[e][s] = v
        self.last_w.clear()
        self.readers.clear()


def build(cfg):
    c = cfg
    D, S, KC, T, TH, NB = c.D, c.S, c.KC, c.T, c.TH, c.NB
    nc = bass.Bass("TRN2", target_bir_lowering=False)
    cx = Ctx(nc)

    def din(name, shape, dt=F32):
        return nc.dram_tensor(name, list(shape), dt, kind="ExternalInput").ap()

    xT = din("xT", [D, S])
    a_w_qkv = din("a_w_qkv", [c.NA, D, c.QKVA])
    a_w_o = din("a_w_o", [c.NA, D, D])
    b_w_qkv = din("b_w_qkv", [max(c.NBL, 1), D, c.QKVB])
    b_w_o = din("b_w_o", [max(c.NBL, 1), D, D])
    mlp_w_in = din("mlp_w_in", [c.DEPTH, D, c.DFF])
    mlp_w_out = din("mlp_w_out", [c.DEPTH, c.DFF, D])
    gains_d = din("gains", [128, (2 * c.DEPTH + 1) * KC])
    sinks_d = din("sinks", [128, c.NA * c.NH])
    cmat_d = din("cmat", [128, 4 * 128], BF16)
    biasA_d = din("biasA", [128, c.NH * 256])
    yT = nc.dram_tensor("yT", [D, S], F32, kind="ExternalOutput").ap()

    xres = nc.dram_tensor("xres", [D, S], F32).ap()
    qT_d = nc.dram_tensor("qT_d", [D, S], BF16).ap()
    kT_d = nc.dram_tensor("kT_d", [D, S], BF16).ap()
    v_d = nc.dram_tensor("v_d", [S, D], BF16).ap()
    attnT_d = nc.dram_tensor("attnT_d", [D, S], BF16).ap()

    gains = nc.alloc_sbuf_tensor("gains_sb", [128, (2 * c.DEPTH + 1) * KC], F32).ap()
    sinks = nc.alloc_sbuf_tensor("sinks_sb", [128, c.NA * c.NH], F32).ap()
    cmat = nc.alloc_sbuf_tensor("cmat_sb", [128, 512], BF16).ap()
    ones_bf = nc.alloc_sbuf_tensor("ones_bf", [128, 128], BF16).ap()
    ident = cmat[:, 0:128]
    negtri = cmat[:, 128:256]
    negsup = cmat[:, 256:384]
    maskdiag = cmat[:, 384:512]

    cx.op("sp", lambda E: E.dma_start(out=gains, in_=gains_d), w=["gains"], dma="c0")
    cx.op("sp", lambda E: E.dma_start(out=sinks, in_=sinks_d), w=["sinks"], dma="c1")
    cx.op("sp", lambda E: E.dma_start(out=cmat, in_=cmat_d), w=["cmat"], dma="c2")
    cx.op("pool", lambda E: E.memset(ones_bf, 1.0), w=["ones"])
    cx.barrier()

    def gain_col(idx, ch):
        return gains[:, idx * KC + ch: idx * KC + ch + 1]

    def tok_phase(L):
        with ExitStack() as es:
            def sb(name, shape, dt):
                return es.enter_context(nc.sbuf_tensor(f"{name}_t{L}", list(shape), dt))[:]

            def ps(name, shape, dt=F32):
                return es.enter_context(nc.psum_tensor(f"{name}_t{L}", list(shape), dt))[:]

            xs = sb("xs", [128, KC, T], F32)
            act16 = sb("act16", [128, KC, T], BF16)
            NW = 4
            wsl = [sb(f"w{i}", [128, c.WS], BF16) for i in range(NW)]
            hbuf = [sb(f"h{i}", [128, 4, T], BF16) for i in range(2)]
            rtmp = [sb(f"rt{i}", [128, 512], F32) for i in range(2)]
            sqt = [sb(f"sq{i}", [128, 512], BF16) for i in range(2)]
            rstd = [sb(f"rs{i}", [128, 512], F32) for i in range(2)]
            stg = [sb(f"stg{i}", [128, T], BF16) for i in range(4)]
            vst = [sb(f"vst{i}", [128, 512], BF16) for i in range(2)]
            yst = [sb(f"yst{i}", [128, 512], F32) for i in range(2)]
            NPS = 6
            pst = [ps(f"ps{i}", [128, 512]) for i in range(NPS)]
            pss = [ps(f"pss{i}", [128, 512]) for i in range(2)]
            cnt = {"w": 0, "ps": 0, "rt": 0, "sq": 0, "stg": 0, "vst": 0, "yst": 0, "pss": 0}

            def nxt(k, n):
                i = cnt[k] % n
                cnt[k] += 1
                return i

            def load_w(src3, nk, ncol):
                i = nxt("w", NW)
                view = wsl[i][:, 0:nk * ncol].rearrange("p (k n) -> p k n", n=ncol)
                cx.op("pool", lambda E: E.dma_start(out=view, in_=src3), w=[f"w{i}"], dma=f"w{i}")
                return view, f"w{i}"

            def wcols(W2, col0, ncol):
                return W2[:, col0:col0 + ncol].rearrange("(k p) n -> p k n", p=128)

            def norm(gidx, dst_fn, dst_keys_fn, final=False):
                for th in range(TH):
                    tsl = slice(th * 512, (th + 1) * 512)
                    pi = nxt("pss", 2)
                    for ch in range(KC):
                        si = nxt("sq", 2)
                        cx.op("act", lambda E, ch=ch, si=si: E.activation(
                            out=sqt[si], in_=xs[:, ch, tsl], func=AF.Square),
                            r=[f"xs{ch}_{th}"], w=[f"sq{si}"])
                        cx.op("pe", lambda E, ch=ch, si=si: E.matmul(
                            pss[pi], ones_bf, sqt[si], start=(ch == 0), stop=(ch == KC - 1),
                            skip_group_check=True),
                            r=[f"sq{si}", "ones"], w=[f"pss{pi}"])
                    ri = th % 2
                    cx.op("act", lambda E: E.activation(
                        out=rstd[ri], in_=pss[pi], func=AF.Sqrt, scale=1.0 / D, bias=EPS),
                        r=[f"pss{pi}"], w=[f"rs{ri}"])
                    cx.op("dve", lambda E: E.reciprocal(out=rstd[ri], in_=rstd[ri]),
                          r=[f"rs{ri}"], w=[f"rs{ri}"])
                    for ch in range(KC):
                        dst = dst_fn(ch, th)
                        cx.op("dve", lambda E, ch=ch, dst=dst: E.scalar_tensor_tensor(
                            out=dst, in0=xs[:, ch, tsl], scalar=gain_col(gidx, ch), in1=rstd[ri],
                            op0=ALU.mult, op1=ALU.mult),
                            r=[f"xs{ch}_{th}", f"rs{ri}", "gains"], w=dst_keys_fn(ch, th))
                        if final:
                            yi = dst_keys_fn(ch, th)[0]
                            cx.op("sp", lambda E, ch=ch, dst=dst: E.dma_start(
                                out=yT_tile[:, ch, tsl], in_=dst), r=[yi], dma=yi)

            def fm_proj(W2, col0, ncol, evac):
                wv, wk = load_w(wcols(W2, col0, ncol), KC, ncol)
                for oc in range(ncol // 128):
                    for th in range(TH):
                        pi = nxt("ps", NPS)
                        tsl = slice(th * 512, (th + 1) * 512)

                        def mm(E, oc=oc, tsl=tsl, pi=pi):
                            out = []
                            for kc in range(KC):
                                out.append(E.matmul(pst[pi], wv[:, kc, oc * 128:(oc + 1) * 128],
                                                    act16[:, kc, tsl], start=(kc == 0),
                                                    stop=(kc == KC - 1), skip_group_check=True))
                            return out
                        cx.op("pe", mm, r=[wk, f"a16_{th}"], w=[f"ps{pi}"])
                        evac(oc, th, pi)

            def resid_add(ch, th, pi):
                tsl = slice(th * 512, (th + 1) * 512)
                cx.op("dve", lambda E: E.tensor_tensor(out=xs[:, ch, tsl], in0=xs[:, ch, tsl],
                                                       in1=pst[pi], op=ALU.add),
                      r=[f"ps{pi}", f"xs{ch}_{th}"], w=[f"xs{ch}_{th}"])

            for tt in range(c.NT):
                t0 = tt * T
                src = xT if L <= 0 else xres
                x_tile = src[:, t0:t0 + T].rearrange("(k p) t -> p k t", p=128)
                xres_tile = xres[:, t0:t0 + T].rearrange("(k p) t -> p k t", p=128)
                yT_tile = yT[:, t0:t0 + T].rearrange("(k p) t -> p k t", p=128)
                xkeys = [f"xs{ch}_{th}" for ch in range(KC) for th in range(TH)]
                akeys = [f"a16_{th}" for th in range(TH)]
                cx.op("sp", lambda E: E.dma_start(out=xs, in_=x_tile), w=xkeys, dma="xs")
                if L >= 0:
                    is_a = (L % 2 == 0)
                    j = L // 2
                    a_tile = attnT_d[:, t0:t0 + T].rearrange("(k p) t -> p k t", p=128)
                    cx.op("sp", lambda E: E.dma_start(out=act16, in_=a_tile), w=akeys, dma="a16")
                    Wo = (a_w_o if is_a else b_w_o)[j]
                    for og in range(D // 512):
                        fm_proj(Wo, og * 512, 512,
                                lambda oc, th, pi, og=og: resid_add(og * 4 + oc, th, pi))
                    norm(c.DEPTH + L, lambda ch, th: act16[:, ch, th * 512:(th + 1) * 512],
                         lambda ch, th: [f"a16_{th}"])
                    Win, Wout = mlp_w_in[L], mlp_w_out[L]
                    NG = c.DFF // 512
                    pend = None

                    def stage2(g, hi, wo, wok):
                        for ch in range(KC):
                            for th in range(TH):
                                pi = nxt("ps", NPS)
                                tsl = slice(th * 512, (th + 1) * 512)

                                def mm(E, ch=ch, tsl=tsl, pi=pi):
                                    out = []
                                    for k4 in range(4):
                                        out.append(E.matmul(pst[pi], wo[:, k4, ch * 128:(ch + 1) * 128],
                                                            hbuf[hi][:, k4, tsl], start=(k4 == 0),
                                                            stop=(k4 == 3), skip_group_check=True))
                                    return out
                                cx.op("pe", mm, r=[wok, f"h{hi}_{th}"], w=[f"ps{pi}"])
                                resid_add(ch, th, pi)

                    for g in range(NG):
                        hi = g % 2
                        wo, wok = load_w(Wout[g * 512:(g + 1) * 512, :].rearrange("(k p) n -> p k n", p=128), 4, D)

                        def ev1(oc, th, pi, hi=hi):
                            tsl = slice(th * 512, (th + 1) * 512)
                            ri = nxt("rt", 2)
                            cx.op("act", lambda E: E.activation(out=rtmp[ri], in_=pst[pi], func=AF.Relu),
                                  r=[f"ps{pi}"], w=[f"rt{ri}"])
                            cx.op("act", lambda E: E.activation(out=hbuf[hi][:, oc, tsl], in_=rtmp[ri],
                                                                func=AF.Square),
                                  r=[f"rt{ri}"], w=[f"h{hi}_{th}"])
                        fm_proj(Win, g * 512, 512, ev1)
                        if pend is not None:
                            stage2(*pend)
                        pend = (g, hi, wo, wok)
                    stage2(*pend)

                if L < c.DEPTH - 1:
                    Ln = L + 1
                    is_a = (Ln % 2 == 0)
                    j = Ln // 2
                    norm(Ln, lambda ch, th: act16[:, ch, th * 512:(th + 1) * 512],
                         lambda ch, th: [f"a16_{th}"])
                    W = (a_w_qkv if is_a else b_w_qkv)[j]
                    dk = c.DKV if is_a else D

                    def qk_groups(col_base, ncols_total, dst, scale):
                        col = 0
                        while col < ncols_total:
                            ncol = min(512, ncols_total - col)

                            def ev(oc, th, pi, col=col):
                                tsl = slice(th * 512, (th + 1) * 512)
                                if th == 0:
                                    ev.si = nxt("stg", 4)
                                si = ev.si
                                cx.op("act", lambda E: E.activation(out=stg[si][:, tsl], in_=pst[pi],
                                                                    func=AF.Copy, scale=scale),
                                      r=[f"ps{pi}"], w=[f"stg{si}"])
                                if th == TH - 1:
                                    row0 = col + oc * 128
                                    cx.op("sp", lambda E: E.dma_start(
                                        out=dst[row0:row0 + 128, t0:t0 + T], in_=stg[si]),
                                        r=[f"stg{si}"], dma=f"stg{si}")
                            fm_proj(W, col_base + col, ncol, ev)
                            col += ncol

                    qk_groups(0, D, qT_d, 1.0 / math.sqrt(HD))
                    qk_groups(D, dk, kT_d, 1.0)
                    col = 0
                    while col < dk:
                        ncol = min(512, dk - col)
                        wv, wk = load_w(wcols(W, D + dk + col, ncol), KC, ncol)
                        for tb in range(T // 128):
                            pi = nxt("ps", NPS)
                            th = tb // 4

                            def mm(E, tb=tb, pi=pi, wv=wv, ncol=ncol):
                                out = []
                                for kc in range(KC):
                                    out.append(E.matmul(pst[pi][:, 0:ncol], act16[:, kc, tb * 128:(tb + 1) * 128],
                                                        wv[:, kc, :], start=(kc == 0), stop=(kc == KC - 1),
                                                        skip_group_check=True))
                                return out
                            cx.op("pe", mm, r=[wk, f"a16_{th}"], w=[f"ps{pi}"])
                            vi = nxt("vst", 2)
                            cx.op("dve", lambda E, pi=pi, vi=vi, ncol=ncol: E.tensor_copy(
                                out=vst[vi][:, 0:ncol], in_=pst[pi][:, 0:ncol]),
                                r=[f"ps{pi}"], w=[f"vst{vi}"])
                            cx.op("sp", lambda E, vi=vi, tb=tb, col=col, ncol=ncol: E.dma_start(
                                out=v_d[t0 + tb * 128:t0 + (tb + 1) * 128, col:col + ncol],
                                in_=vst[vi][:, 0:ncol]), r=[f"vst{vi}"], dma=f"vst{vi}")
                        col += ncol
                    if L >= 0:
                        cx.op("sp", lambda E: E.dma_start(out=xres_tile, in_=xs), r=xkeys, dma="xst")
                else:
                    def ydst(ch, th):
                        yi = nxt("yst", 2)
                        ydst.last = yi
                        return yst[yi]
                    norm(2 * c.DEPTH, ydst, lambda ch, th: [f"yst{ydst.last}"], final=True)
            cx.barrier()

    def att_b(j):
        with ExitStack() as es:
            def sb(name, shape, dt):
                return es.enter_context(nc.sbuf_tensor(f"{name}_b{j}", list(shape), dt))[:]

            def ps(name, shape, dt=F32):
                return es.enter_context(nc.psum_tensor(f"{name}_b{j}", list(shape), dt))[:]

            Kt = [sb(f"Kt{i}", [128, S], BF16) for i in range(2)]
            Qt = [sb(f"Qt{i}", [128, S], BF16) for i in range(2)]
            Vraw = [sb(f"Vr{i}", [128, NB, 128], BF16) for i in range(2)]
            Vpad = [sb(f"Vp{i}", [128, NB, 2, 128], BF16) for i in range(2)]
            e_sb = [[sb(f"e{h}{i}", [128, 512], F32) for i in range(2)] for h in range(2)]
            sp_sb = [[sb(f"sp{h}{i}", [128, 512], BF16) for i in range(2)] for h in range(2)]
            ec_sb = [[sb(f"ec{h}{i}", [128, 512], F32) for i in range(2)] for h in range(2)]
            a_sb = [[sb(f"a{h}{i}", [128, 512], BF16) for i in range(2)] for h in range(2)]
            ost = [sb(f"ost{i}", [128, 512], BF16) for i in range(2)]
            Z = [[ps(f"Z{h}{i}", [128, 512]) for i in range(2)] for h in range(2)]
            P = [ps(f"P{h}", [128, 512]) for h in range(2)]
            O = [ps(f"O{i}", [128, 512]) for i in range(2)]

            for i in range(2):
                cx.op("pool", lambda E, i=i: E.memset(Vpad[i], 0.0), w=[f"Vp{i}"])

            step = 0
            for hp in range(c.NH // 2):
                bi = hp % 2
                rows = slice(hp * 128, (hp + 1) * 128)
                cx.op("sp", lambda E: E.dma_start(out=Kt[bi], in_=kT_d[rows, :]), w=[f"Kt{bi}"], dma=f"Kt{bi}")
                cx.op("sp", lambda E: E.dma_start(out=Qt[bi], in_=qT_d[rows, :]), w=[f"Qt{bi}"], dma=f"Qt{bi}")
                cx.op("sp", lambda E: E.dma_start(
                    out=Vraw[bi], in_=v_d[:, rows].rearrange("(kb p) f -> p kb f", p=128)),
                    w=[f"Vr{bi}"], dma=f"Vr{bi}")
                cx.op("pool", lambda E: E.tensor_copy(out=Vpad[bi][:, :, 0, 0:64], in_=Vraw[bi][:, :, 0:64]),
                      r=[f"Vr{bi}"], w=[f"Vp{bi}"])
                cx.op("pool", lambda E: E.tensor_copy(out=Vpad[bi][:, :, 1, 64:128], in_=Vraw[bi][:, :, 64:128]),
                      r=[f"Vr{bi}"], w=[f"Vp{bi}"])
                for g in range(NB // 4):
                    oi = g % 2
                    q0 = g * 512
                    for kb in range(4 * g + 3, -1, -1):
                        diag = kb >= 4 * g
                        first = kb == 4 * g + 3
                        c0 = (kb - 4 * g) * 128 if diag else 0
                        N = 512 - c0
                        si = step % 2
                        step += 1
                        ksl = slice(kb * 128, (kb + 1) * 128)
                        for hh in range(2):
                            prt = slice(hh * 64, (hh + 1) * 64)

                            def mmz(E, hh=hh, prt=prt):
                                out = [E.matmul(Z[hh][si][:, c0:512], Kt[bi][prt, ksl],
                                                Qt[bi][prt, q0 + c0:q0 + 512], start=True, stop=not diag,
                                                skip_group_check=True)]
                                if diag:
                                    out.append(E.matmul(Z[hh][si][:, c0:c0 + 128], ident, maskdiag,
                                                        start=False, stop=True, skip_group_check=True))
                                return out
                            cx.op("pe", mmz, r=[f"Kt{bi}", f"Qt{bi}", "cmat"], w=[f"Z{hh}{si}"])
                        for hh in range(2):
                            cx.op("act", lambda E, hh=hh: E.activation(
                                out=e_sb[hh][si][:, c0:512], in_=Z[hh][si][:, c0:512], func=AF.Exp),
                                r=[f"Z{hh}{si}"], w=[f"e{hh}{si}"])
                        for hh in range(2):
                            cx.op("act", lambda E, hh=hh: E.activation(
                                out=sp_sb[hh][si][:, c0:512], in_=e_sb[hh][si][:, c0:512], func=AF.Ln,
                                bias=1.0, scale=1.0),
                                r=[f"e{hh}{si}"], w=[f"sp{hh}{si}"])
                        for hh in range(2):
                            def mmp1(E, hh=hh):
                                return [E.matmul(P[hh][:, c0:512], negtri, sp_sb[hh][si][:, c0:512],
                                                 start=first, stop=True, skip_group_check=True)]
                            cx.op("pe", mmp1, r=[f"sp{hh}{si}", "cmat"], w=[f"P{hh}"])
                        for hh in range(2):
                            cx.op("act", lambda E, hh=hh: E.activation(
                                out=ec_sb[hh][si][:, c0:512], in_=P[hh][:, c0:512], func=AF.Exp),
                                r=[f"P{hh}"], w=[f"ec{hh}{si}"])
                        for hh in range(2):
                            cx.op("dve", lambda E, hh=hh: E.tensor_tensor(
                                out=a_sb[hh][si][:, c0:512], in0=e_sb[hh][si][:, c0:512],
                                in1=ec_sb[hh][si][:, c0:512], op=ALU.mult),
                                r=[f"e{hh}{si}", f"ec{hh}{si}"], w=[f"a{hh}{si}"])
                        for hh in range(2):
                            def mmo(E, hh=hh):
                                lhs = Vpad[bi][:, kb, hh, :]
                                return [E.matmul(O[oi][:, c0:512], lhs, a_sb[hh][si][:, c0:512],
                                                 start=(first and hh == 0), stop=True, skip_group_check=True),
                                        E.matmul(P[hh][:, c0:512], negsup, sp_sb[hh][si][:, c0:512],
                                                 start=False, stop=True, skip_group_check=True)]
                            cx.op("pe", mmo, r=[f"a{hh}{si}", f"Vp{bi}", f"sp{hh}{si}", "cmat"],
                                  w=[f"O{oi}", f"P{hh}"])
                    cx.op("dve", lambda E: E.tensor_copy(out=ost[oi], in_=O[oi]), r=[f"O{oi}"], w=[f"ost{oi}"])
                    cx.op("sp", lambda E: E.dma_start(out=attnT_d[rows, q0:q0 + 512], in_=ost[oi]),
                          r=[f"ost{oi}"], dma=f"ost{oi}")
            cx.barrier()

    def att_a(j):
        with ExitStack() as es:
            def sb(name, shape, dt):
                return es.enter_context(nc.sbuf_tensor(f"{name}_a{j}", list(shape), dt))[:]

            def ps(name, shape, dt=F32):
                return es.enter_context(nc.psum_tensor(f"{name}_a{j}", list(shape), dt))[:]

            biasA = sb("biasA", [128, c.NH, 256], F32)
            Kt = [sb(f"Kt{i}", [128, S], BF16) for i in range(2)]
            Qt = [sb(f"Qt{i}", [128, S], BF16) for i in range(2)]
            Vraw = [sb(f"Vr{i}", [128, NB, 64], BF16) for i in range(2)]
            Vpad = [sb(f"Vp{i}", [128, NB, 2, 128], BF16) for i in range(2)]
            orow = [sb(f"orow{i}", [128, S], BF16) for i in range(2)]
            NS = 3
            s_sb = [sb(f"s{i}", [128, 256], F32) for i in range(NS)]
            p_sb = [sb(f"p{i}", [128, 256], F32) for i in range(NS)]
            pn_sb = [sb(f"pn{i}", [128, 256], BF16) for i in range(NS)]
            pT_sb = [sb(f"pT{i}", [128, 256], BF16) for i in range(NS)]
            sm = [sb(f"sm{i}", [128, 8], F32) for i in range(NS)]
            Sps = [ps(f"S{i}", [128, 512]) for i in range(3)]
            PTps = [ps(f"PT{i}", [128, 1024], BF16) for i in range(2)]
            Ops = [ps(f"O{i}", [128, 512]) for i in range(2)]

            cx.op("sp", lambda E: E.dma_start(out=biasA, in_=biasA_d.rearrange("p (h k) -> p h k", k=256)),
                  w=["biasA"], dma="biasA")
            for i in range(2):
                cx.op("pool", lambda E, i=i: E.memset(Vpad[i], 0.0), w=[f"Vp{i}"])
            u = 0
            cidx = 0
            for kvh in range(c.NKV):
                ki = kvh % 2
                krow = slice(kvh * 64, (kvh + 1) * 64)
                cx.op("sp", lambda E: [E.dma_start(out=Kt[ki][0:64, :], in_=kT_d[krow, :]),
                                       E.dma_start(out=Kt[ki][64:128, :], in_=kT_d[krow, :])],
                      w=[f"Kt{ki}"], dma=f"Kt{ki}")
                cx.op("sp", lambda E: E.dma_start(
                    out=Vraw[ki], in_=v_d[:, krow].rearrange("(kb p) f -> p kb f", p=128)),
                    w=[f"Vr{ki}"], dma=f"Vr{ki}")
                cx.op("pool", lambda E: E.tensor_copy(out=Vpad[ki][:, :, 0, 0:64], in_=Vraw[ki]),
                      r=[f"Vr{ki}"], w=[f"Vp{ki}"])
                cx.op("pool", lambda E: E.tensor_copy(out=Vpad[ki][:, :, 1, 64:128], in_=Vraw[ki]),
                      r=[f"Vr{ki}"], w=[f"Vp{ki}"])
                for cc in range(4):
                    ch = kvh * 4 + cc
                    qi = cidx % 2
                    cidx += 1
                    rows = slice(ch * 128, (ch + 1) * 128)
                    cx.op("sp", lambda E: E.dma_start(out=Qt[qi], in_=qT_d[rows, :]), w=[f"Qt{qi}"], dma=f"Qt{qi}")
                    for jb in range(NB):
                        oi = jb % 2
                        qsl = slice(jb * 128, (jb + 1) * 128)
                        nkb = 1 if jb == 0 else 2
                        N = nkb * 128
                        k0 = (jb - nkb + 1) * 128
                        b0 = 256 - N
                        for hh in range(2):
                            h = 2 * ch + hh
                            prt = slice(hh * 64, (hh + 1) * 64)
                            ui = u % NS
                            zi = u % 3
                            ti = u % 2
                            u += 1
                            cx.op("pe", lambda E: E.matmul(Sps[zi][:, 0:N], Qt[qi][prt, qsl], Kt[ki][prt, k0:k0 + N],
                                                           start=True, stop=True, skip_group_check=True),
                                  r=[f"Qt{qi}", f"Kt{ki}"], w=[f"S{zi}"])
                            cx.op("dve", lambda E: E.tensor_tensor(out=s_sb[ui][:, 0:N], in0=Sps[zi][:, 0:N],
                                                                   in1=biasA[:, h, b0:256], op=ALU.add),
                                  r=[f"S{zi}", "biasA"], w=[f"s{ui}"])
                            cx.op("dve", lambda E: E.tensor_reduce(out=sm[ui][:, 0:1], in_=s_sb[ui][:, 0:N],
                                                                   axis=AX.X, op=ALU.max),
                                  r=[f"s{ui}"], w=[f"sm{ui}"])
                            sk = sinks[:, j * c.NH + h: j * c.NH + h + 1]
                            cx.op("dve", lambda E: E.tensor_scalar(out=sm[ui][:, 1:2], in0=sm[ui][:, 0:1],
                                                                   scalar1=sk, scalar2=-1.0,
                                                                   op0=ALU.max, op1=ALU.mult),
                                  r=[f"sm{ui}", "sinks"], w=[f"sm{ui}"])
                            cx.op("act", lambda E: E.activation(out=p_sb[ui][:, 0:N], in_=s_sb[ui][:, 0:N],
                                                                func=AF.Exp, bias=sm[ui][:, 1:2], scale=1.0),
                                  r=[f"s{ui}", f"sm{ui}"], w=[f"p{ui}"])
                            cx.op("act", lambda E: E.activation(out=sm[ui][:, 2:3], in_=sk, func=AF.Exp,
                                                                bias=sm[ui][:, 1:2], scale=1.0),
                                  r=[f"sm{ui}", "sinks"], w=[f"sm{ui}"])
                            cx.op("dve", lambda E: E.tensor_reduce(out=sm[ui][:, 3:4], in_=p_sb[ui][:, 0:N],
                                                                   axis=AX.X, op=ALU.add),
                                  r=[f"p{ui}", f"sm{ui}"], w=[f"sm{ui}"])
                            cx.op("dve", lambda E: E.tensor_tensor(out=sm[ui][:, 4:5], in0=sm[ui][:, 3:4],
                                                                   in1=sm[ui][:, 2:3], op=ALU.add),
                                  r=[f"sm{ui}"], w=[f"sm{ui}"])
                            cx.op("dve", lambda E: E.reciprocal(out=sm[ui][:, 5:6], in_=sm[ui][:, 4:5]),
                                  r=[f"sm{ui}"], w=[f"sm{ui}"])
                            cx.op("dve", lambda E: E.tensor_scalar(out=pn_sb[ui][:, 0:N], in0=p_sb[ui][:, 0:N],
                                                                   scalar1=sm[ui][:, 5:6], scalar2=None,
                                                                   op0=ALU.mult),
                                  r=[f"p{ui}", f"sm{ui}"], w=[f"pn{ui}"])

                            def tr(E):
                                return [E.transpose(PTps[ti][:, b * 128:(b + 1) * 128],
                                                    pn_sb[ui][:, b * 128:(b + 1) * 128], ident)
                                        for b in range(nkb)]
                            cx.op("pe", tr, r=[f"pn{ui}", "cmat"], w=[f"PT{ti}"])
                            cx.op("act", lambda E: E.activation(out=pT_sb[ui][:, 0:N], in_=PTps[ti][:, 0:N],
                                                                func=AF.Copy),
                                  r=[f"PT{ti}"], w=[f"pT{ui}"])

                            def mmo(E):
                                out = []
                                for b in range(nkb):
                                    kbi = jb - nkb + 1 + b
                                    out.append(E.matmul(Ops[oi][:, 0:128], Vpad[ki][:, kbi, hh, :],
                                                        pT_sb[ui][:, b * 128:(b + 1) * 128],
                                                        start=(hh == 0 and b == 0),
                                                        stop=(hh == 1 and b == nkb - 1), skip_group_check=True))
                                return out
                            cx.op("pe", mmo, r=[f"pT{ui}", f"Vp{ki}"], w=[f"O{oi}"])
                        cx.op("dve", lambda E: E.tensor_copy(out=orow[qi][:, qsl], in_=Ops[oi][:, 0:128]),
                              r=[f"O{oi}"], w=[f"orow{qi}"])
                    cx.op("sp", lambda E: E.dma_start(out=attnT_d[rows, :], in_=orow[qi]),
                          r=[f"orow{qi}"], dma=f"orow{qi}")
            cx.barrier()

    tok_phase(-1)
    for L in range(c.DEPTH):
        if L % 2 == 0:
            att_a(L // 2)
        else:
            att_b(L // 2)
        tok_phase(L)
    return nc, cx


def host_consts(cfg):
    c = cfg
    i = np.arange(128)
    ident = np.eye(128, dtype=np.float32)
    negtri = -(i[:, None] >= i[None, :]).astype(np.float32)
    negsup = -(i[:, None] < i[None, :]).astype(np.float32)
    maskdiag = np.where(i[:, None] < i[None, :], 0.0, NEG).astype(np.float32)
    cmat = np.concatenate([ident, negtri, negsup, maskdiag], axis=1).astype(ml_dtypes.bfloat16)
    q = np.arange(128)[:, None]
    k = np.arange(256)[None, :]
    dist = (q + 128) - k
    valid = (dist >= 0) & (dist < 128)
    slopes = np.power(2.0, -8.0 * (np.arange(c.NH, dtype=np.float32) + 1.0) / c.NH).astype(np.float32)
    bias = -slopes[None, :, None] * dist[:, None, :].astype(np.float32)
    bias = np.where(valid[:, None, :], bias, NEG).astype(np.float32)
    return cmat, np.ascontiguousarray(bias.reshape(128, c.NH * 256))


def fm(v, KC):
    return np.ascontiguousarray(v.reshape(KC, 128).T)


_CACHE = {}
CORE_OF_BATCH = [0, 2, 4, 6]


def run(cfg, inputs, n_cores=8, core_of_batch=None):
    c = cfg
    x = np.asarray(inputs["x"], dtype=np.float32)
    B = x.shape[0]
    if core_of_batch is None:
        core_of_batch = CORE_OF_BATCH[:B]
    key = (c.D, c.S, c.DEPTH, c.T)
    if key not in _CACHE:
        _CACHE[key] = build(c)[0]
    nc = _CACHE[key]
    cmat, biasA = host_consts(c)
    gains = np.concatenate(
        [fm(np.asarray(inputs["norm_mix"][i], np.float32), c.KC) for i in range(c.DEPTH)]
        + [fm(np.asarray(inputs["norm_mlp"][i], np.float32), c.KC) for i in range(c.DEPTH)]
        + [fm(np.asarray(inputs["final_norm"], np.float32), c.KC)], axis=1)
    sinks = np.ascontiguousarray(np.broadcast_to(
        np.asarray(inputs["a_sinks"], np.float32).reshape(1, -1), (128, c.NA * c.NH)))
    shared = {
        "a_w_qkv": np.ascontiguousarray(inputs["a_w_qkv"], dtype=np.float32),
        "a_w_o": np.ascontiguousarray(inputs["a_w_o"], dtype=np.float32),
        "b_w_qkv": np.ascontiguousarray(inputs["b_w_qkv"], dtype=np.float32),
        "b_w_o": np.ascontiguousarray(inputs["b_w_o"], dtype=np.float32),
        "mlp_w_in": np.ascontiguousarray(inputs["mlp_w_in"], dtype=np.float32),
        "mlp_w_out": np.ascontiguousarray(inputs["mlp_w_out"], dtype=np.float32),
        "gains": np.ascontiguousarray(gains), "sinks": sinks, "cmat": cmat, "biasA": biasA,
    }
    zero_x = np.zeros((c.D, c.S), np.float32)
    in_maps = []
    for core in range(n_cores):
        m = dict(shared)
        if core in core_of_batch:
            b = core_of_batch.index(core)
            m["xT"] = np.ascontiguousarray(x[b].T)
        else:
            m["xT"] = zero_x
        in_maps.append(m)
    res = run_bass_kernel_spmd(nc, in_maps, core_ids=list(range(n_cores)))
    out = np.empty((B, c.S, c.D), np.float32)
    for b, core in enumerate(core_of_batch):
        out[b] = np.asarray(res.results[core]["yT"], dtype=np.float32).T
    return out


def kernel(x, a_w_qkv, a_w_o, a_sinks, b_w_qkv, b_w_o, norm_mix, norm_mlp,
           mlp_w_in, mlp_w_out, final_norm):
    cfg = Cfg(D=2048, S=4096, DEPTH=4, T=1024)
    return run(cfg, dict(x=x, a_w_qkv=a_w_qkv, a_w_o=a_w_o, a_sinks=a_sinks, b_w_qkv=b_w_qkv,
                         b_w_o=b_w_o, norm_mix=norm_mix, norm_mlp=norm_mlp, mlp_w_in=mlp_w_in,
                         mlp_w_out=mlp_w_out, final_norm=final_norm))
```

```python
import math
from contextlib import ExitStack

import numpy as np
import ml_dtypes

import concourse.bass as bass
import concourse.mybir as mybir
from concourse.bass_utils import run_bass_kernel_spmd

F32 = mybir.dt.float32
BF16 = mybir.dt.bfloat16
AF = mybir.ActivationFunctionType
ALU = mybir.AluOpType
AX = mybir.AxisListType

HD = 64
BLK = 128
EPS = 1e-5
NEG = -30000.0
SEM_ROT = 16000


class Cfg:
    def __init__(self, D=2048, S=4096, DEPTH=4, T=1024):
        self.D, self.S, self.DEPTH, self.T = D, S, DEPTH, T
        self.KC = D // 128
        self.NH = D // HD
        self.NKV = self.NH // 8
        self.DKV = self.NKV * HD
        self.DFF = 4 * D
        self.NB = S // BLK
        self.NT = S // T
        self.TH = T // 512
        self.NA = (DEPTH + 1) // 2
        self.NBL = DEPTH // 2
        self.QKVA = D + 2 * self.DKV
        self.QKVB = 3 * D
        self.WS = 4 * D


class Ctx:
    def __init__(self, nc):
        self.nc = nc
        self.eng = {"pe": nc.tensor, "act": nc.scalar, "dve": nc.vector,
                    "pool": nc.gpsimd, "sp": nc.sync}
        self.sems = []
        self.cur = {}
        self.pe_sems = set()
        self.waited = {e: {} for e in self.eng}
        self.last_w = {}
        self.readers = {}
        self.dma = {}
        self.nops = 0

    def _newsem(self):
        self.sems.append(self.nc.alloc_semaphore(f"s{len(self.sems)}"))
        return len(self.sems) - 1

    def _tok(self, e):
        c = self.cur.get(e)
        if c is None or c[1] >= SEM_ROT:
            c = [self._newsem(), 0]
            self.cur[e] = c
            if e == "pe":
                self.pe_sems.add(c[0])
        c[1] += 1
        return (c[0], c[1])

    def op(self, e, fn, r=(), w=(), dma=None):
        deps = {}

        def add(tok):
            if tok is not None and deps.get(tok[0], 0) < tok[1]:
                deps[tok[0]] = tok[1]

        for k in r:
            add(self.last_w.get(k))
        for k in w:
            add(self.last_w.get(k))
            for tok in self.readers.get(k, {}).values():
                add(tok)
        d = None
        if dma is not None:
            d = self.dma.get(dma)
            if d is None:
                d = self.dma[dma] = [self._newsem(), 0]
            if d[1] > 0:
                add((d[0], d[1]))
        E = self.eng[e]
        for s, v in deps.items():
            if e == "pe" and s in self.pe_sems:
                continue
            if self.waited[e].get(s, 0) >= v:
                continue
            E.wait_ge(self.sems[s], v)
            self.waited[e][s] = v
        insts = fn(E)
        if not isinstance(insts, (list, tuple)):
            insts = [insts]
        self.nops += len(insts)
        if dma is not None:
            for i in insts:
                i.then_inc(self.sems[d[0]], 16)
                d[1] += 16
            tok = (d[0], d[1])
            rkey = ("dma", dma)
        else:
            tok = self._tok(e)
            insts[-1].then_inc(self.sems[tok[0]], 1)
            rkey = e
        for k in r:
            self.readers.setdefault(k, {})[rkey] = tok
        for k in w:
            self.last_w[k] = tok
            self.readers[k] = {}
        return tok

    def all_tokens(self):
        toks = [(c[0], c[1]) for c in self.cur.values()]
        toks += [(d[0], d[1]) for d in self.dma.values() if d[1] > 0]
        return toks

    def barrier(self, engines=None):
        toks = self.all_tokens()
        for e, E in self.eng.items():
            if engines is not None and e not in engines:
                continue
            for s, v in toks:
                if self.waited[e].get(s, 0) < v:
                    E.wait_ge(self.sems[s], v)
                    self.waited[e][s] = v
        self.last_w.clear()
        self.readers.clear()


def build(cfg):
    c = cfg
    D, S, KC, T, TH, NB = c.D, c.S, c.KC, c.T, c.TH, c.NB
    nc = bass.Bass("TRN2", target_bir_lowering=False)
    cx = Ctx(nc)

    def din(name, shape, dt=F32):
        return nc.dram_tensor(name, list(shape), dt, kind="ExternalInput").ap()

    xT = din("xT", [D, S])
    a_w_qkv = din("a_w_qkv", [c.NA, D, c.QKVA])
    a_w_o = din("a_w_o", [c.NA, D, D])
    b_w_qkv = din("b_w_qkv", [max(c.NBL, 1), D, c.QKVB])
    b_w_o = din("b_w_o", [max(c.NBL, 1), D, D])
    mlp_w_in = din("mlp_w_in", [c.DEPTH, D, c.DFF])
    mlp_w_out = din("mlp_w_out", [c.DEPTH, c.DFF, D])
    gains_d = din("gains", [128, (2 * c.DEPTH + 1) * KC])
    sinks_d = din("sinks", [128, c.NA * c.NH])
    cmat_d = din("cmat", [128, 4 * 128], BF16)
    biasA_d = din("biasA", [128, c.NH * 256])
    yT = nc.dram_tensor("yT", [D, S], F32, kind="ExternalOutput").ap()

    xres = nc.dram_tensor("xres", [D, S], F32).ap()
    qT_d = nc.dram_tensor("qT_d", [D, S], BF16).ap()
    kT_d = nc.dram_tensor("kT_d", [D, S], BF16).ap()
    v_d = nc.dram_tensor("v_d", [S, D], BF16).ap()
    attnT_d = nc.dram_tensor("attnT_d", [D, S], BF16).ap()

    gains = nc.alloc_sbuf_tensor("gains_sb", [128, (2 * c.DEPTH + 1) * KC], F32).ap()
    sinks = nc.alloc_sbuf_tensor("sinks_sb", [128, c.NA * c.NH], F32).ap()
    cmat = nc.alloc_sbuf_tensor("cmat_sb", [128, 512], BF16).ap()
    ones_bf = nc.alloc_sbuf_tensor("ones_bf", [128, 128], BF16).ap()
    ident = cmat[:, 0:128]
    negtri = cmat[:, 128:256]
    negsup = cmat[:, 256:384]
    maskdiag = cmat[:, 384:512]

    cx.op("sp", lambda E: E.dma_start(out=gains, in_=gains_d), w=["gains"], dma="c0")
    cx.op("sp", lambda E: E.dma_start(out=sinks, in_=sinks_d), w=["sinks"], dma="c1")
    cx.op("sp", lambda E: E.dma_start(out=cmat, in_=cmat_d), w=["cmat"], dma="c2")
    cx.op("pool", lambda E: E.memset(ones_bf, 1.0), w=["ones"])
    cx.barrier()

    def gain_col(idx, ch):
        return gains[:, idx * KC + ch: idx * KC + ch + 1]

    def tok_phase(L):
        with ExitStack() as es:
            def sb(name, shape, dt):
                return es.enter_context(nc.sbuf_tensor(f"{name}_t{L}", list(shape), dt))[:]

            def ps(name, shape, dt=F32):
                return es.enter_context(nc.psum_tensor(f"{name}_t{L}", list(shape), dt))[:]

            xs = sb("xs", [128, KC, T], F32)
            act16 = sb("act16", [128, KC, T], BF16)
            NW = 4
            wsl = [sb(f"w{i}", [128, c.WS], BF16) for i in range(NW)]
            hbuf = [sb(f"h{i}", [128, 4, T], BF16) for i in range(2)]
            rtmp = [sb(f"rt{i}", [128, 512], F32) for i in range(2)]
            sqt = [sb(f"sq{i}", [128, 512], BF16) for i in range(2)]
            rstd = [sb(f"rs{i}", [128, 512], F32) for i in range(2)]
            stg = [sb(f"stg{i}", [128, T], BF16) for i in range(4)]
            vst = [sb(f"vst{i}", [128, 512], BF16) for i in range(2)]
            yst = [sb(f"yst{i}", [128, 512], F32) for i in range(2)]
            NPS = 6
            pst = [ps(f"ps{i}", [128, 512]) for i in range(NPS)]
            pss = [ps(f"pss{i}", [128, 512]) for i in range(2)]
            cnt = {"w": 0, "ps": 0, "rt": 0, "sq": 0, "stg": 0, "vst": 0, "yst": 0, "pss": 0}

            def nxt(k, n):
                i = cnt[k] % n
                cnt[k] += 1
                return i

            def load_w(src3, nk, ncol):
                i = nxt("w", NW)
                view = wsl[i][:, 0:nk * ncol].rearrange("p (k n) -> p k n", n=ncol)
                cx.op("pool", lambda E: E.dma_start(out=view, in_=src3), w=[f"w{i}"], dma=f"w{i}")
                return view, f"w{i}"

            def wcols(W2, col0, ncol):
                return W2[:, col0:col0 + ncol].rearrange("(k p) n -> p k n", p=128)

            def norm(gidx, dst_fn, dst_keys_fn, final=False):
                for th in range(TH):
                    tsl = slice(th * 512, (th + 1) * 512)
                    pi = nxt("pss", 2)
                    for ch in range(KC):
                        si = nxt("sq", 2)
                        cx.op("act", lambda E, ch=ch, si=si: E.activation(
                            out=sqt[si], in_=xs[:, ch, tsl], func=AF.Square),
                            r=[f"xs{ch}_{th}"], w=[f"sq{si}"])
                        cx.op("pe", lambda E, ch=ch, si=si: E.matmul(
                            pss[pi], ones_bf, sqt[si], start=(ch == 0), stop=(ch == KC - 1),
                            skip_group_check=True),
                            r=[f"sq{si}", "ones"], w=[f"pss{pi}"])
                    ri = th % 2
                    cx.op("act", lambda E: E.activation(
                        out=rstd[ri], in_=pss[pi], func=AF.Sqrt, scale=1.0 / D, bias=EPS),
                        r=[f"pss{pi}"], w=[f"rs{ri}"])
                    cx.op("dve", lambda E: E.reciprocal(out=rstd[ri], in_=rstd[ri]),
                          r=[f"rs{ri}"], w=[f"rs{ri}"])
                    for ch in range(KC):
                        dst = dst_fn(ch, th)
                        cx.op("dve", lambda E, ch=ch, dst=dst: E.scalar_tensor_tensor(
                            out=dst, in0=xs[:, ch, tsl], scalar=gain_col(gidx, ch), in1=rstd[ri],
                            op0=ALU.mult, op1=ALU.mult),
                            r=[f"xs{ch}_{th}", f"rs{ri}", "gains"], w=dst_keys_fn(ch, th))
                        if final:
                            yi = dst_keys_fn(ch, th)[0]
                            cx.op("sp", lambda E, ch=ch, dst=dst: E.dma_start(
                                out=yT_tile[:, ch, tsl], in_=dst), r=[yi], dma=yi)

            def fm_proj(W2, col0, ncol, evac):
                wv, wk = load_w(wcols(W2, col0, ncol), KC, ncol)
                for oc in range(ncol // 128):
                    for th in range(TH):
                        pi = nxt("ps", NPS)
                        tsl = slice(th * 512, (th + 1) * 512)

                        def mm(E, oc=oc, tsl=tsl, pi=pi):
                            out = []
                            for kc in range(KC):
                                out.append(E.matmul(pst[pi], wv[:, kc, oc * 128:(oc + 1) * 128],
                                                    act16[:, kc, tsl], start=(kc == 0),
                                                    stop=(kc == KC - 1), skip_group_check=True))
                            return out
                        cx.op("pe", mm, r=[wk, f"a16_{th}"], w=[f"ps{pi}"])
                        evac(oc, th, pi)

            def resid_add(ch, th, pi):
                tsl = slice(th * 512, (th + 1) * 512)
                cx.op("dve", lambda E: E.tensor_tensor(out=xs[:, ch, tsl], in0=xs[:, ch, tsl],
                                                       in1=pst[pi], op=ALU.add),
                      r=[f"ps{pi}", f"xs{ch}_{th}"], w=[f"xs{ch}_{th}"])

            for tt in range(c.NT):
                t0 = tt * T
                src = xT if L <= 0 else xres
                x_tile = src[:, t0:t0 + T].rearrange("(k p) t -> p k t", p=128)
                xres_tile = xres[:, t0:t0 + T].rearrange("(k p) t -> p k t", p=128)
                yT_tile = yT[:, t0:t0 + T].rearrange("(k p) t -> p k t", p=128)
                xkeys = [f"xs{ch}_{th}" for ch in range(KC) for th in range(TH)]
                akeys = [f"a16_{th}" for th in range(TH)]
                cx.op("sp", lambda E: E.dma_start(out=xs, in_=x_tile), w=xkeys, dma="xs")
                if L >= 0:
                    is_a = (L % 2 == 0)
                    j = L // 2
                    a_tile = attnT_d[:, t0:t0 + T].rearrange("(k p) t -> p k t", p=128)
                    cx.op("sp", lambda E: E.dma_start(out=act16, in_=a_tile), w=akeys, dma="a16")
                    Wo = (a_w_o if is_a else b_w_o)[j]
                    for og in range(D // 512):
                        fm_proj(Wo, og * 512, 512,
                                lambda oc, th, pi, og=og: resid_add(og * 4 + oc, th, pi))
                    norm(c.DEPTH + L, lambda ch, th: act16[:, ch, th * 512:(th + 1) * 512],
                         lambda ch, th: [f"a16_{th}"])
                    Win, Wout = mlp_w_in[L], mlp_w_out[L]
                    NG = c.DFF // 512
                    pend = None

                    def stage2(g, hi, wo, wok):
                        for ch in range(KC):
                            for th in range(TH):
                                pi = nxt("ps", NPS)
                                tsl = slice(th * 512, (th + 1) * 512)

                                def mm(E, ch=ch, tsl=tsl, pi=pi):
                                    out = []
                                    for k4 in range(4):
                                        out.append(E.matmul(pst[pi], wo[:, k4, ch * 128:(ch + 1) * 128],
                                                            hbuf[hi][:, k4, tsl], start=(k4 == 0),
                                                            stop=(k4 == 3), skip_group_check=True))
                                    return out
                                cx.op("pe", mm, r=[wok, f"h{hi}_{th}"], w=[f"ps{pi}"])
                                resid_add(ch, th, pi)

                    for g in range(NG):
                        hi = g % 2
                        wo, wok = load_w(Wout[g * 512:(g + 1) * 512, :].rearrange("(k p) n -> p k n", p=128), 4, D)

                        def ev1(oc, th, pi, hi=hi):
                            tsl = slice(th * 512, (th + 1) * 512)
                            ri = nxt("rt", 2)
                            cx.op("act", lambda E: E.activation(out=rtmp[ri], in_=pst[pi], func=AF.Relu),
                                  r=[f"ps{pi}"], w=[f"rt{ri}"])
                            cx.op("act", lambda E: E.activation(out=hbuf[hi][:, oc, tsl], in_=rtmp[ri],
                                                                func=AF.Square),
                                  r=[f"rt{ri}"], w=[f"h{hi}_{th}"])
                        fm_proj(Win, g * 512, 512, ev1)
                        if pend is not None:
                            stage2(*pend)
                        pend = (g, hi, wo, wok)
                    stage2(*pend)

                if L < c.DEPTH - 1:
                    Ln = L + 1
                    is_a = (Ln % 2 == 0)
                    j = Ln // 2
                    norm(Ln, lambda ch, th: act16[:, ch, th * 512:(th + 1) * 512],
                         lambda ch, th: [f"a16_{th}"])
                    W = (a_w_qkv if is_a else b_w_qkv)[j]
                    dk = c.DKV if is_a else D

                    def qk_groups(col_base, ncols_total, dst, scale):
                        col = 0
                        while col < ncols_total:
                            ncol = min(512, ncols_total - col)

                            def ev(oc, th, pi, col=col):
                                tsl = slice(th * 512, (th + 1) * 512)
                                if th == 0:
                                    ev.si = nxt("stg", 4)
                                si = ev.si
                                cx.op("act", lambda E: E.activation(out=stg[si][:, tsl], in_=pst[pi],
                                                                    func=AF.Copy, scale=scale),
                                      r=[f"ps{pi}"], w=[f"stg{si}"])
                                if th == TH - 1:
                                    row0 = col + oc * 128
                                    cx.op("sp", lambda E: E.dma_start(
                                        out=dst[row0:row0 + 128, t0:t0 + T], in_=stg[si]),
                                        r=[f"stg{si}"], dma=f"stg{si}")
                            fm_proj(W, col_base + col, ncol, ev)
                            col += ncol

                    qk_groups(0, D, qT_d, 1.0 / math.sqrt(HD))
                    qk_groups(D, dk, kT_d, 1.0)
                    col = 0
                    while col < dk:
                        ncol = min(512, dk - col)
                        wv, wk = load_w(wcols(W, D + dk + col, ncol), KC, ncol)
                        for tb in range(T // 128):
                            pi = nxt("ps", NPS)
                            th = tb // 4

                            def mm(E, tb=tb, pi=pi, wv=wv, ncol=ncol):
                                out = []
                                for kc in range(KC):
                                    out.append(E.matmul(pst[pi][:, 0:ncol], act16[:, kc, tb * 128:(tb + 1) * 128],
                                                        wv[:, kc, :], start=(kc == 0), stop=(kc == KC - 1),
                                                        skip_group_check=True))
                                return out
                            cx.op("pe", mm, r=[wk, f"a16_{th}"], w=[f"ps{pi}"])
                            vi = nxt("vst", 2)
                            cx.op("dve", lambda E, pi=pi, vi=vi, ncol=ncol: E.tensor_copy(
                                out=vst[vi][:, 0:ncol], in_=pst[pi][:, 0:ncol]),
                                r=[f"ps{pi}"], w=[f"vst{vi}"])
                            cx.op("sp", lambda E, vi=vi, tb=tb, col=col, ncol=ncol: E.dma_start(
                                out=v_d[t0 + tb * 128:t0 + (tb + 1) * 128, col:col + ncol],
                                in_=vst[vi][:, 0:ncol]), r=[f"vst{vi}"], dma=f"vst{vi}")
                        col += ncol
                    if L >= 0:
                        cx.op("sp", lambda E: E.dma_start(out=xres_tile, in_=xs), r=xkeys, dma="xst")
                else:
                    def ydst(ch, th):
                        yi = nxt("yst", 2)
                        ydst.last = yi
                        return yst[yi]
                    norm(2 * c.DEPTH, ydst, lambda ch, th: [f"yst{ydst.last}"], final=True)
            cx.barrier()

    def att_b(j):
        with ExitStack() as es:
            def sb(name, shape, dt):
                return es.enter_context(nc.sbuf_tensor(f"{name}_b{j}", list(shape), dt))[:]

            def ps(name, shape, dt=F32):
                return es.enter_context(nc.psum_tensor(f"{name}_b{j}", list(shape), dt))[:]

            Kt = [sb(f"Kt{i}", [128, S], BF16) for i in range(2)]
            Qt = [sb(f"Qt{i}", [128, S], BF16) for i in range(2)]
            Vraw = [sb(f"Vr{i}", [128, NB, 128], BF16) for i in range(2)]
            Vpad = [sb(f"Vp{i}", [128, NB, 2, 128], BF16) for i in range(2)]
            NSL = 3
            e_sb = [[sb(f"e{h}{i}", [128, 512], F32) for i in range(NSL)] for h in range(2)]
            sp_sb = [[sb(f"sp{h}{i}", [128, 512], BF16) for i in range(NSL)] for h in range(2)]
            ec_sb = [[sb(f"ec{h}{i}", [128, 512], F32) for i in range(2)] for h in range(2)]
            a_sb = [[sb(f"a{h}{i}", [128, 512], BF16) for i in range(NSL)] for h in range(2)]
            ost = [sb(f"ost{i}", [128, 512], BF16) for i in range(2)]
            Z = [[ps(f"Z{h}{i}", [128, 512]) for i in range(2)] for h in range(2)]
            P = [ps(f"P{h}", [128, 512]) for h in range(2)]
            O = [ps(f"O{i}", [128, 512]) for i in range(2)]

            for i in range(2):
                cx.op("pool", lambda E, i=i: E.memset(Vpad[i], 0.0), w=[f"Vp{i}"])

            steps = []
            for hp in range(c.NH // 2):
                for g in range(NB // 4):
                    for kb in range(4 * g + 3, -1, -1):
                        steps.append(dict(hp=hp, bi=hp % 2, g=g, oi=g % 2, kb=kb,
                                          first=(kb == 4 * g + 3), last=(kb == 0),
                                          new_hp=(g == 0 and kb == 3)))
            for n, st in enumerate(steps):
                st["zi"] = n % 2
                st["si"] = n % NSL

            def loads(hp):
                bi = hp % 2
                rows = slice(hp * 128, (hp + 1) * 128)
                cx.op("sp", lambda E: E.dma_start(out=Kt[bi], in_=kT_d[rows, :]), w=[f"Kt{bi}"], dma=f"Kt{bi}")
                cx.op("sp", lambda E: E.dma_start(out=Qt[bi], in_=qT_d[rows, :]), w=[f"Qt{bi}"], dma=f"Qt{bi}")
                cx.op("sp", lambda E: E.dma_start(
                    out=Vraw[bi], in_=v_d[:, rows].rearrange("(kb p) f -> p kb f", p=128)),
                    w=[f"Vr{bi}"], dma=f"Vr{bi}")
                cx.op("pool", lambda E: E.tensor_copy(out=Vpad[bi][:, :, 0, 0:64], in_=Vraw[bi][:, :, 0:64]),
                      r=[f"Vr{bi}"], w=[f"Vp{bi}"])
                cx.op("pool", lambda E: E.tensor_copy(out=Vpad[bi][:, :, 1, 64:128], in_=Vraw[bi][:, :, 64:128]),
                      r=[f"Vr{bi}"], w=[f"Vp{bi}"])

            def geom(st):
                g, kb = st["g"], st["kb"]
                diag = kb >= 4 * g
                c0 = (kb - 4 * g) * 128 if diag else 0
                return diag, c0, g * 512, slice(kb * 128, (kb + 1) * 128)

            def stage_a(st):
                diag, c0, q0, ksl = geom(st)
                bi, zi, si = st["bi"], st["zi"], st["si"]
                for hh in range(2):
                    prt = slice(hh * 64, (hh + 1) * 64)

                    def mmz(E, hh=hh, prt=prt):
                        out = [E.matmul(Z[hh][zi][:, c0:512], Kt[bi][prt, ksl],
                                        Qt[bi][prt, q0 + c0:q0 + 512], start=True, stop=not diag,
                                        skip_group_check=True)]
                        if diag:
                            out.append(E.matmul(Z[hh][zi][:, c0:c0 + 128], ident, maskdiag,
                                                start=False, stop=True, skip_group_check=True))
                        return out
                    cx.op("pe", mmz, r=[f"Kt{bi}", f"Qt{bi}", "cmat"], w=[f"Z{hh}{zi}"])
                for hh in range(2):
                    cx.op("act", lambda E, hh=hh: E.activation(
                        out=e_sb[hh][si][:, c0:512], in_=Z[hh][zi][:, c0:512], func=AF.Exp),
                        r=[f"Z{hh}{zi}"], w=[f"e{hh}{si}"])
                for hh in range(2):
                    cx.op("act", lambda E, hh=hh: E.activation(
                        out=sp_sb[hh][si][:, c0:512], in_=e_sb[hh][si][:, c0:512], func=AF.Ln,
                        bias=1.0, scale=1.0),
                        r=[f"e{hh}{si}"], w=[f"sp{hh}{si}"])

            def stage_b(st, prev):
                diag, c0, q0, ksl = geom(st)
                si, ei = st["si"], st["zi"]
                first = st["first"]
                for hh in range(2):
                    cx.op("pe", lambda E, hh=hh: E.matmul(
                        P[hh][:, c0:512], negtri, sp_sb[hh][si][:, c0:512],
                        start=first, stop=True, skip_group_check=True),
                        r=[f"sp{hh}{si}", "cmat"], w=[f"P{hh}"])
                for hh in range(2):
                    cx.op("act", lambda E, hh=hh: E.activation(
                        out=ec_sb[hh][ei][:, c0:512], in_=P[hh][:, c0:512], func=AF.Exp),
                        r=[f"P{hh}"], w=[f"ec{hh}{ei}"])
                for hh in range(2):
                    cx.op("dve", lambda E, hh=hh: E.tensor_tensor(
                        out=a_sb[hh][si][:, c0:512], in0=e_sb[hh][si][:, c0:512],
                        in1=ec_sb[hh][ei][:, c0:512], op=ALU.mult),
                        r=[f"e{hh}{si}", f"ec{hh}{ei}"], w=[f"a{hh}{si}"])
                if prev is not None:
                    stage_o(prev)
                for hh in range(2):
                    cx.op("pe", lambda E, hh=hh: E.matmul(
                        P[hh][:, c0:512], negsup, sp_sb[hh][si][:, c0:512],
                        start=False, stop=True, skip_group_check=True),
                        r=[f"sp{hh}{si}", "cmat"], w=[f"P{hh}"])

            def stage_o(st):
                diag, c0, q0, ksl = geom(st)
                bi, oi, si, kb = st["bi"], st["oi"], st["si"], st["kb"]
                for hh in range(2):
                    cx.op("pe", lambda E, hh=hh: E.matmul(
                        O[oi][:, c0:512], Vpad[bi][:, kb, hh, :], a_sb[hh][si][:, c0:512],
                        start=(st["first"] and hh == 0), stop=True, skip_group_check=True),
                        r=[f"a{hh}{si}", f"Vp{bi}"], w=[f"O{oi}"])
                if st["last"]:
                    rows = slice(st["hp"] * 128, (st["hp"] + 1) * 128)
                    cx.op("dve", lambda E: E.tensor_copy(out=ost[oi], in_=O[oi]), r=[f"O{oi}"], w=[f"ost{oi}"])
                    cx.op("sp", lambda E: E.dma_start(out=attnT_d[rows, q0:q0 + 512], in_=ost[oi]),
                          r=[f"ost{oi}"], dma=f"ost{oi}")

            loads(0)
            ns = len(steps)
            pending_load = None
            stage_a(steps[0])
            for n in range(ns):
                if pending_load is not None:
                    loads(pending_load)
                    pending_load = None
                if n + 1 < ns:
                    stage_a(steps[n + 1])
                stage_b(steps[n], steps[n - 1] if n >= 1 else None)
                if n >= 1 and steps[n - 1]["new_hp"] and steps[n - 1]["hp"] + 1 < c.NH // 2:
                    pending_load = steps[n - 1]["hp"] + 1
            stage_o(steps[ns - 1])
            cx.barrier()


    def att_a(j):
        with ExitStack() as es:
            def sb(name, shape, dt):
                return es.enter_context(nc.sbuf_tensor(f"{name}_a{j}", list(shape), dt))[:]

            def ps(name, shape, dt=F32):
                return es.enter_context(nc.psum_tensor(f"{name}_a{j}", list(shape), dt))[:]

            biasA = sb("biasA", [128, c.NH, 256], F32)
            Kt = [sb(f"Kt{i}", [128, S], BF16) for i in range(2)]
            Qt = [sb(f"Qt{i}", [128, S], BF16) for i in range(2)]
            Vraw = [sb(f"Vr{i}", [128, NB, 64], BF16) for i in range(2)]
            Vpad = [sb(f"Vp{i}", [128, NB, 2, 128], BF16) for i in range(2)]
            orow = [sb(f"orow{i}", [128, S], BF16) for i in range(2)]
            s_sb = [sb(f"s{i}", [128, 8, 256], F32) for i in range(2)]
            p_sb = [sb(f"p{i}", [128, 8, 256], F32) for i in range(2)]
            pn_sb = [sb(f"pn{i}", [128, 8, 256], BF16) for i in range(2)]
            pT_sb = [sb(f"pT{i}", [128, 8, 256], BF16) for i in range(2)]
            sm = [sb(f"sm{i}", [128, 6, 8], F32) for i in range(2)]
            Sps = [ps(f"S{i}", [128, 2, 256]) for i in range(4)]
            PTps = [ps(f"PT{i}", [128, 4, 256], BF16) for i in range(2)]
            Ops = [ps(f"O{i}", [128, 512]) for i in range(2)]

            cx.op("sp", lambda E: E.dma_start(out=biasA, in_=biasA_d.rearrange("p (h k) -> p h k", k=256)),
                  w=["biasA"], dma="biasA")
            for i in range(2):
                cx.op("pool", lambda E, i=i: E.memset(Vpad[i], 0.0), w=[f"Vp{i}"])
            batches = [[0]] + [list(range(b0, min(b0 + 4, NB))) for b0 in range(1, NB, 4)]
            nbatch = 0
            cidx = 0
            for kvh in range(c.NKV):
                ki = kvh % 2
                krow = slice(kvh * 64, (kvh + 1) * 64)
                cx.op("sp", lambda E: [E.dma_start(out=Kt[ki][0:64, :], in_=kT_d[krow, :]),
                                       E.dma_start(out=Kt[ki][64:128, :], in_=kT_d[krow, :])],
                      w=[f"Kt{ki}"], dma=f"Kt{ki}")
                cx.op("sp", lambda E: E.dma_start(
                    out=Vraw[ki], in_=v_d[:, krow].rearrange("(kb p) f -> p kb f", p=128)),
                    w=[f"Vr{ki}"], dma=f"Vr{ki}")
                cx.op("pool", lambda E: E.tensor_copy(out=Vpad[ki][:, :, 0, 0:64], in_=Vraw[ki]),
                      r=[f"Vr{ki}"], w=[f"Vp{ki}"])
                cx.op("pool", lambda E: E.tensor_copy(out=Vpad[ki][:, :, 1, 64:128], in_=Vraw[ki]),
                      r=[f"Vr{ki}"], w=[f"Vp{ki}"])
                for cc in range(4):
                    ch = kvh * 4 + cc
                    qi = cidx % 2
                    cidx += 1
                    rows = slice(ch * 128, (ch + 1) * 128)
                    cx.op("sp", lambda E: E.dma_start(out=Qt[qi], in_=qT_d[rows, :]), w=[f"Qt{qi}"], dma=f"Qt{qi}")
                    for blks in batches:
                        bp = nbatch % 2
                        nbatch += 1
                        nq = len(blks)
                        nkb = 1 if blks[0] == 0 else 2
                        N = nkb * 128
                        b0 = 256 - N
                        units = [(hh, qq) for hh in range(2) for qq in range(nq)]
                        nu = len(units)

                        def uidx(hh, qq):
                            return hh * nq + qq
                        for hh, qq in units:
                            u = uidx(hh, qq)
                            jb = blks[qq]
                            prt = slice(hh * 64, (hh + 1) * 64)
                            k0 = (jb - nkb + 1) * 128
                            sbk = hh * 2 + qq // 2
                            cx.op("pe", lambda E, sbk=sbk, qq=qq, jb=jb, prt=prt, k0=k0: E.matmul(
                                Sps[sbk][:, qq % 2, 0:N], Qt[qi][prt, jb * 128:(jb + 1) * 128],
                                Kt[ki][prt, k0:k0 + N], start=True, stop=True, skip_group_check=True),
                                r=[f"Qt{qi}", f"Kt{ki}"], w=[f"S{sbk}"])
                        for hh, qq in units:
                            u = uidx(hh, qq)
                            h = 2 * ch + hh
                            sbk = hh * 2 + qq // 2
                            cx.op("dve", lambda E, u=u, h=h, sbk=sbk, qq=qq: E.tensor_tensor(
                                out=s_sb[bp][:, u, 0:N], in0=Sps[sbk][:, qq % 2, 0:N],
                                in1=biasA[:, h, b0:256], op=ALU.add),
                                r=[f"S{sbk}", "biasA"], w=[f"s{bp}_{u}"])
                        cx.op("dve", lambda E: E.tensor_reduce(out=sm[bp][:, 0, 0:nu], in_=s_sb[bp][:, 0:nu, 0:N],
                                                               axis=AX.X, op=ALU.max),
                              r=[f"s{bp}_{u}" for u in range(nu)], w=[f"sm{bp}_0"])
                        for hh in range(2):
                            h = 2 * ch + hh
                            sk = sinks[:, j * c.NH + h: j * c.NH + h + 1]
                            cx.op("dve", lambda E, hh=hh, sk=sk: E.tensor_scalar(
                                out=sm[bp][:, 1, hh * nq:(hh + 1) * nq], in0=sm[bp][:, 0, hh * nq:(hh + 1) * nq],
                                scalar1=sk, scalar2=-1.0, op0=ALU.max, op1=ALU.mult),
                                r=[f"sm{bp}_0", "sinks"], w=[f"sm{bp}_1"])
                        for hh, qq in units:
                            u = uidx(hh, qq)
                            cx.op("act", lambda E, u=u: E.activation(
                                out=p_sb[bp][:, u, 0:N], in_=s_sb[bp][:, u, 0:N], func=AF.Exp,
                                bias=sm[bp][:, 1, u:u + 1], scale=1.0),
                                r=[f"s{bp}_{u}", f"sm{bp}_1"], w=[f"p{bp}_{u}"])
                        for hh in range(2):
                            h = 2 * ch + hh
                            sk = sinks[:, j * c.NH + h: j * c.NH + h + 1]
                            cx.op("act", lambda E, hh=hh, sk=sk: E.activation(
                                out=sm[bp][:, 2, hh * nq:(hh + 1) * nq], in_=sm[bp][:, 1, hh * nq:(hh + 1) * nq],
                                func=AF.Exp, bias=sk, scale=1.0),
                                r=[f"sm{bp}_1", "sinks"], w=[f"sm{bp}_2"])
                        cx.op("dve", lambda E: E.tensor_reduce(out=sm[bp][:, 3, 0:nu], in_=p_sb[bp][:, 0:nu, 0:N],
                                                               axis=AX.X, op=ALU.add),
                              r=[f"p{bp}_{u}" for u in range(nu)], w=[f"sm{bp}_3"])
                        cx.op("dve", lambda E: E.tensor_tensor(out=sm[bp][:, 4, 0:nu], in0=sm[bp][:, 3, 0:nu],
                                                               in1=sm[bp][:, 2, 0:nu], op=ALU.add),
                              r=[f"sm{bp}_3", f"sm{bp}_2"], w=[f"sm{bp}_4"])
                        cx.op("dve", lambda E: E.reciprocal(out=sm[bp][:, 5, 0:nu], in_=sm[bp][:, 4, 0:nu]),
                              r=[f"sm{bp}_4"], w=[f"sm{bp}_5"])
                        for hh, qq in units:
                            u = uidx(hh, qq)
                            cx.op("dve", lambda E, u=u: E.tensor_scalar(
                                out=pn_sb[bp][:, u, 0:N], in0=p_sb[bp][:, u, 0:N],
                                scalar1=sm[bp][:, 5, u:u + 1], scalar2=None, op0=ALU.mult),
                                r=[f"p{bp}_{u}", f"sm{bp}_5"], w=[f"pn{bp}_{u}"])
                        for half in range((nu + 3) // 4):
                            us = list(range(half * 4, min(half * 4 + 4, nu)))

                            def tr(E, us=us, half=half):
                                out = []
                                for u in us:
                                    for b in range(nkb):
                                        out.append(E.transpose(PTps[half][:, u % 4, b * 128:(b + 1) * 128],
                                                               pn_sb[bp][:, u, b * 128:(b + 1) * 128], ident))
                                return out
                            cx.op("pe", tr, r=[f"pn{bp}_{u}" for u in us] + ["cmat"], w=[f"PT{half}"])
                            cx.op("act", lambda E, us=us, half=half: E.activation(
                                out=pT_sb[bp][:, us[0]:us[-1] + 1, 0:N], in_=PTps[half][:, 0:len(us), 0:N],
                                func=AF.Copy),
                                r=[f"PT{half}"], w=[f"pT{bp}_{half}"])
                        oi = bp

                        def mmo(E):
                            out = []
                            firstmm = True
                            for hh, qq in units:
                                u = uidx(hh, qq)
                                jb = blks[qq]
                                for b in range(nkb):
                                    kbi = jb - nkb + 1 + b
                                    out.append(E.matmul(Ops[oi][:, qq * 128:(qq + 1) * 128],
                                                        Vpad[ki][:, kbi, hh, :],
                                                        pT_sb[bp][:, u, b * 128:(b + 1) * 128],
                                                        start=firstmm, stop=True, skip_group_check=True))
                                    firstmm = False
                            return out
                        cx.op("pe", mmo, r=[f"pT{bp}_{hf}" for hf in range((nu + 3) // 4)] + [f"Vp{ki}"], w=[f"O{oi}"])
                        cx.op("dve", lambda E: E.tensor_copy(
                            out=orow[qi][:, blks[0] * 128:(blks[-1] + 1) * 128], in_=Ops[oi][:, 0:nq * 128]),
                            r=[f"O{oi}"], w=[f"orow{qi}"])
                    cx.op("sp", lambda E: E.dma_start(out=attnT_d[rows, :], in_=orow[qi]),
                          r=[f"orow{qi}"], dma=f"orow{qi}")
            cx.barrier()


    tok_phase(-1)
    for L in range(c.DEPTH):
        if L % 2 == 0:
            att_a(L // 2)
        else:
            att_b(L // 2)
        tok_phase(L)
    return nc, cx


def host_consts(cfg):
    c = cfg
    i = np.arange(128)
    ident = np.eye(128, dtype=np.float32)
    negtri = -(i[:, None] >= i[None, :]).astype(np.float32)
    negsup = -(i[:, None] < i[None, :]).astype(np.float32)
    maskdiag = np.where(i[:, None] < i[None, :], 0.0, NEG).astype(np.float32)
    cmat = np.concatenate([ident, negtri, negsup, maskdiag], axis=1).astype(ml_dtypes.bfloat16)
    q = np.arange(128)[:, None]
    k = np.arange(256)[None, :]
    dist = (q + 128) - k
    valid = (dist >= 0) & (dist < 128)
    slopes = np.power(2.0, -8.0 * (np.arange(c.NH, dtype=np.float32) + 1.0) / c.NH).astype(np.float32)
    bias = -slopes[None, :, None] * dist[:, None, :].astype(np.float32)
    bias = np.where(valid[:, None, :], bias, NEG).astype(np.float32)
    return cmat, np.ascontiguousarray(bias.reshape(128, c.NH * 256))


def fm(v, KC):
    return np.ascontiguousarray(v.reshape(KC, 128).T)


_CACHE = {}
CORE_OF_BATCH = [0, 2, 4, 6]


def run(cfg, inputs, n_cores=8, core_of_batch=None):
    c = cfg
    x = np.asarray(inputs["x"], dtype=np.float32)
    B = x.shape[0]
    if core_of_batch is None:
        core_of_batch = CORE_OF_BATCH[:B]
    key = (c.D, c.S, c.DEPTH, c.T)
    if key not in _CACHE:
        _CACHE[key] = build(c)[0]
    nc = _CACHE[key]
    cmat, biasA = host_consts(c)
    gains = np.concatenate(
        [fm(np.asarray(inputs["norm_mix"][i], np.float32), c.KC) for i in range(c.DEPTH)]
        + [fm(np.asarray(inputs["norm_mlp"][i], np.float32), c.KC) for i in range(c.DEPTH)]
        + [fm(np.asarray(inputs["final_norm"], np.float32), c.KC)], axis=1)
    sinks = np.ascontiguousarray(np.broadcast_to(
        np.asarray(inputs["a_sinks"], np.float32).reshape(1, -1), (128, c.NA * c.NH)))
    shared = {
        "a_w_qkv": np.ascontiguousarray(inputs["a_w_qkv"], dtype=np.float32),
        "a_w_o": np.ascontiguousarray(inputs["a_w_o"], dtype=np.float32),
        "b_w_qkv": np.ascontiguousarray(inputs["b_w_qkv"], dtype=np.float32),
        "b_w_o": np.ascontiguousarray(inputs["b_w_o"], dtype=np.float32),
        "mlp_w_in": np.ascontiguousarray(inputs["mlp_w_in"], dtype=np.float32),
        "mlp_w_out": np.ascontiguousarray(inputs["mlp_w_out"], dtype=np.float32),
        "gains": np.ascontiguousarray(gains), "sinks": sinks, "cmat": cmat, "biasA": biasA,
    }
    zero_x = np.zeros((c.D, c.S), np.float32)
    in_maps = []
    for core in range(n_cores):
        m = dict(shared)
        if core in core_of_batch:
            b = core_of_batch.index(core)
            m["xT"] = np.ascontiguousarray(x[b].T)
        else:
            m["xT"] = zero_x
        in_maps.append(m)
    res = run_bass_kernel_spmd(nc, in_maps, core_ids=list(range(n_cores)))
    out = np.empty((B, c.S, c.D), np.float32)
    for b, core in enumerate(core_of_batch):
        out[b] = np.asarray(res.results[core]["yT"], dtype=np.float32).T
    return out


def kernel(x, a_w_qkv, a_w_o, a_sinks, b_w_qkv, b_w_o, norm_mix, norm_mlp,
           mlp_w_in, mlp_w_out, final_norm):
    cfg = Cfg(D=2048, S=4096, DEPTH=4, T=1024)
    return run(cfg, dict(x=x, a_w_qkv=a_w_qkv, a_w_o=a_w_o, a_sinks=a_sinks, b_w_qkv=b_w_qkv,
                         b_w_o=b_w_o, norm_mix=norm_mix, norm_mlp=norm_mlp, mlp_w_in=mlp_w_in,
                         mlp_w_out=mlp_w_out, final_norm=final_norm))
```

```python
import math
from contextlib import ExitStack

import numpy as np
import ml_dtypes

import concourse.bass as bass
import concourse.mybir as mybir
from concourse.bass_utils import run_bass_kernel_spmd

F32 = mybir.dt.float32
BF16 = mybir.dt.bfloat16
AF = mybir.ActivationFunctionType
ALU = mybir.AluOpType
AX = mybir.AxisListType

HD = 64
BLK = 128
EPS = 1e-5
NEG = -30000.0
SEM_ROT = 16000


class Cfg:
    def __init__(self, D=2048, S=4096, DEPTH=4, T=1024):
        self.D, self.S, self.DEPTH, self.T = D, S, DEPTH, T
        self.KC = D // 128
        self.NH = D // HD
        self.NKV = self.NH // 8
        self.DKV = self.NKV * HD
        self.DFF = 4 * D
        self.NB = S // BLK
        self.NT = S // T
        self.TH = T // 512
        self.NA = (DEPTH + 1) // 2
        self.NBL = DEPTH // 2
        self.QKVA = D + 2 * self.DKV
        self.QKVB = 3 * D
        self.WS = 4 * D


class Ctx:
    def __init__(self, nc):
        self.nc = nc
        self.eng = {"pe": nc.tensor, "act": nc.scalar, "dve": nc.vector,
                    "pool": nc.gpsimd, "sp": nc.sync}
        self.sems = []
        self.cur = {}
        self.pe_sems = set()
        self.waited = {e: {} for e in self.eng}
        self.last_w = {}
        self.readers = {}
        self.dma = {}
        self.nops = 0

    def _newsem(self):
        self.sems.append(self.nc.alloc_semaphore(f"s{len(self.sems)}"))
        return len(self.sems) - 1

    def _tok(self, e):
        c = self.cur.get(e)
        if c is None or c[1] >= SEM_ROT:
            c = [self._newsem(), 0]
            self.cur[e] = c
            if e == "pe":
                self.pe_sems.add(c[0])
        c[1] += 1
        return (c[0], c[1])

    def op(self, e, fn, r=(), w=(), dma=None):
        deps = {}

        def add(tok):
            if tok is not None and deps.get(tok[0], 0) < tok[1]:
                deps[tok[0]] = tok[1]

        for k in r:
            add(self.last_w.get(k))
        for k in w:
            add(self.last_w.get(k))
            for tok in self.readers.get(k, {}).values():
                add(tok)
        d = None
        if dma is not None:
            d = self.dma.get(dma)
            if d is None:
                d = self.dma[dma] = [self._newsem(), 0]
            if d[1] > 0:
                add((d[0], d[1]))
        E = self.eng[e]
        for s, v in deps.items():
            if e == "pe" and s in self.pe_sems:
                continue
            if self.waited[e].get(s, 0) >= v:
                continue
            E.wait_ge(self.sems[s], v)
            self.waited[e][s] = v
        insts = fn(E)
        if not isinstance(insts, (list, tuple)):
            insts = [insts]
        self.nops += len(insts)
        if dma is not None:
            for i in insts:
                i.then_inc(self.sems[d[0]], 16)
                d[1] += 16
            tok = (d[0], d[1])
            rkey = ("dma", dma)
        else:
            tok = self._tok(e)
            insts[-1].then_inc(self.sems[tok[0]], 1)
            rkey = e
        for k in r:
            self.readers.setdefault(k, {})[rkey] = tok
        for k in w:
            self.last_w[k] = tok
            self.readers[k] = {}
        return tok

    def all_tokens(self):
        toks = [(c[0], c[1]) for c in self.cur.values()]
        toks += [(d[0], d[1]) for d in self.dma.values() if d[1] > 0]
        return toks

    def barrier(self, engines=None):
        toks = self.all_tokens()
        for e, E in self.eng.items():
            if engines is not None and e not in engines:
                continue
            for s, v in toks:
                if self.waited[e].get(s, 0) < v:
                    E.wait_ge(self.sems[s], v)
                    self.waited[e][s] = v
        self.last_w.clear()
        self.readers.clear()


def build(cfg):
    c = cfg
    D, S, KC, T, TH, NB = c.D, c.S, c.KC, c.T, c.TH, c.NB
    nc = bass.Bass("TRN2", target_bir_lowering=False)
    cx = Ctx(nc)

    def din(name, shape, dt=F32):
        return nc.dram_tensor(name, list(shape), dt, kind="ExternalInput").ap()

    xT = din("xT", [D, S])
    a_w_qkv = din("a_w_qkv", [c.NA, D, c.QKVA])
    a_w_o = din("a_w_o", [c.NA, D, D])
    b_w_qkv = din("b_w_qkv", [max(c.NBL, 1), D, c.QKVB])
    b_w_o = din("b_w_o", [max(c.NBL, 1), D, D])
    mlp_w_in = din("mlp_w_in", [c.DEPTH, D, c.DFF])
    mlp_w_out = din("mlp_w_out", [c.DEPTH, c.DFF, D])
    gains_d = din("gains", [128, (2 * c.DEPTH + 1) * KC])
    sinks_d = din("sinks", [128, c.NA * c.NH])
    cmat_d = din("cmat", [128, 4 * 128], BF16)
    biasA_d = din("biasA", [128, c.NH * 256])
    yT = nc.dram_tensor("yT", [D, S], F32, kind="ExternalOutput").ap()

    xres = nc.dram_tensor("xres", [D, S], F32).ap()
    qT_d = nc.dram_tensor("qT_d", [D, S], BF16).ap()
    kT_d = nc.dram_tensor("kT_d", [D, S], BF16).ap()
    v_d = nc.dram_tensor("v_d", [S, D], BF16).ap()
    attnT_d = nc.dram_tensor("attnT_d", [D, S], BF16).ap()

    gains = nc.alloc_sbuf_tensor("gains_sb", [128, (2 * c.DEPTH + 1) * KC], F32).ap()
    sinks = nc.alloc_sbuf_tensor("sinks_sb", [128, c.NA * c.NH], F32).ap()
    cmat = nc.alloc_sbuf_tensor("cmat_sb", [128, 512], BF16).ap()
    ones_bf = nc.alloc_sbuf_tensor("ones_bf", [128, 128], BF16).ap()
    ident = cmat[:, 0:128]
    negtri = cmat[:, 128:256]
    negsup = cmat[:, 256:384]
    maskdiag = cmat[:, 384:512]

    cx.op("sp", lambda E: E.dma_start(out=gains, in_=gains_d), w=["gains"], dma="c0")
    cx.op("sp", lambda E: E.dma_start(out=sinks, in_=sinks_d), w=["sinks"], dma="c1")
    cx.op("sp", lambda E: E.dma_start(out=cmat, in_=cmat_d), w=["cmat"], dma="c2")
    cx.op("pool", lambda E: E.memset(ones_bf, 1.0), w=["ones"])
    cx.barrier()

    def gain_col(idx, ch):
        return gains[:, idx * KC + ch: idx * KC + ch + 1]

    def tok_phase(L):
        with ExitStack() as es:
            def sb(name, shape, dt):
                return es.enter_context(nc.sbuf_tensor(f"{name}_t{L}", list(shape), dt))[:]

            def ps(name, shape, dt=F32):
                return es.enter_context(nc.psum_tensor(f"{name}_t{L}", list(shape), dt))[:]

            xs = sb("xs", [128, KC, T], F32)
            act16 = sb("act16", [128, KC, T], BF16)
            NW = 4
            wsl = [sb(f"w{i}", [128, c.WS], BF16) for i in range(NW)]
            hbuf = [sb(f"h{i}", [128, 4, T], BF16) for i in range(2)]
            rtmp = [sb(f"rt{i}", [128, 512], F32) for i in range(2)]
            sqt = [sb(f"sq{i}", [128, 512], BF16) for i in range(2)]
            rstd = [sb(f"rs{i}", [128, 512], F32) for i in range(2)]
            stg = [sb(f"stg{i}", [128, T], BF16) for i in range(4)]
            vst = [sb(f"vst{i}", [128, 512], BF16) for i in range(2)]
            yst = [sb(f"yst{i}", [128, 512], F32) for i in range(2)]
            NPS = 6
            pst = [ps(f"ps{i}", [128, 512]) for i in range(NPS)]
            pss = [ps(f"pss{i}", [128, 512]) for i in range(2)]
            cnt = {"w": 0, "ps": 0, "rt": 0, "sq": 0, "stg": 0, "vst": 0, "yst": 0, "pss": 0}

            def nxt(k, n):
                i = cnt[k] % n
                cnt[k] += 1
                return i

            def load_w(src3, nk, ncol):
                i = nxt("w", NW)
                view = wsl[i][:, 0:nk * ncol].rearrange("p (k n) -> p k n", n=ncol)
                cx.op("pool", lambda E: E.dma_start(out=view, in_=src3), w=[f"w{i}"], dma=f"w{i}")
                return view, f"w{i}"

            def load_w_slot(i, src3, nk, ncol):
                view = wsl[i][:, 0:nk * ncol].rearrange("p (k n) -> p k n", n=ncol)
                cx.op("pool", lambda E: E.dma_start(out=view, in_=src3), w=[f"w{i}"], dma=f"w{i}")
                return view, f"w{i}"

            def wcols(W2, col0, ncol):
                return W2[:, col0:col0 + ncol].rearrange("(k p) n -> p k n", p=128)

            def norm(gidx, dst_fn, dst_keys_fn, final=False):
                for th in range(TH):
                    tsl = slice(th * 512, (th + 1) * 512)
                    pi = nxt("pss", 2)
                    for ch in range(KC):
                        si = nxt("sq", 2)
                        cx.op("act", lambda E, ch=ch, si=si: E.activation(
                            out=sqt[si], in_=xs[:, ch, tsl], func=AF.Square),
                            r=[f"xs{ch}_{th}"], w=[f"sq{si}"])
                        cx.op("pe", lambda E, ch=ch, si=si: E.matmul(
                            pss[pi], ones_bf, sqt[si], start=(ch == 0), stop=(ch == KC - 1),
                            skip_group_check=True),
                            r=[f"sq{si}", "ones"], w=[f"pss{pi}"])
                    ri = th % 2
                    cx.op("act", lambda E: E.activation(
                        out=rstd[ri], in_=pss[pi], func=AF.Sqrt, scale=1.0 / D, bias=EPS),
                        r=[f"pss{pi}"], w=[f"rs{ri}"])
                    cx.op("dve", lambda E: E.reciprocal(out=rstd[ri], in_=rstd[ri]),
                          r=[f"rs{ri}"], w=[f"rs{ri}"])
                    for ch in range(KC):
                        dst = dst_fn(ch, th)
                        cx.op("dve", lambda E, ch=ch, dst=dst: E.scalar_tensor_tensor(
                            out=dst, in0=xs[:, ch, tsl], scalar=gain_col(gidx, ch), in1=rstd[ri],
                            op0=ALU.mult, op1=ALU.mult),
                            r=[f"xs{ch}_{th}", f"rs{ri}", "gains"], w=dst_keys_fn(ch, th))
                        if final:
                            yi = dst_keys_fn(ch, th)[0]
                            cx.op("sp", lambda E, ch=ch, dst=dst: E.dma_start(
                                out=yT_tile[:, ch, tsl], in_=dst), r=[yi], dma=yi)

            def fm_proj(W2, col0, ncol, evac, pre=None):
                wv, wk = pre if pre is not None else load_w(wcols(W2, col0, ncol), KC, ncol)
                for oc in range(ncol // 128):
                    for th in range(TH):
                        pi = nxt("ps", NPS)
                        tsl = slice(th * 512, (th + 1) * 512)

                        def mm(E, oc=oc, tsl=tsl, pi=pi):
                            out = []
                            for kc in range(KC):
                                out.append(E.matmul(pst[pi], wv[:, kc, oc * 128:(oc + 1) * 128],
                                                    act16[:, kc, tsl], start=(kc == 0),
                                                    stop=(kc == KC - 1), skip_group_check=True))
                            return out
                        cx.op("pe", mm, r=[wk, f"a16_{th}"], w=[f"ps{pi}"])
                        evac(oc, th, pi)

            def resid_add(ch, th, pi):
                tsl = slice(th * 512, (th + 1) * 512)
                cx.op("dve", lambda E: E.tensor_tensor(out=xs[:, ch, tsl], in0=xs[:, ch, tsl],
                                                       in1=pst[pi], op=ALU.add),
                      r=[f"ps{pi}", f"xs{ch}_{th}"], w=[f"xs{ch}_{th}"])

            for tt in range(c.NT):
                t0 = tt * T
                src = xT if L <= 0 else xres
                x_tile = src[:, t0:t0 + T].rearrange("(k p) t -> p k t", p=128)
                xres_tile = xres[:, t0:t0 + T].rearrange("(k p) t -> p k t", p=128)
                yT_tile = yT[:, t0:t0 + T].rearrange("(k p) t -> p k t", p=128)
                xkeys = [f"xs{ch}_{th}" for ch in range(KC) for th in range(TH)]
                akeys = [f"a16_{th}" for th in range(TH)]
                cx.op("sp", lambda E: E.dma_start(out=xs, in_=x_tile), w=xkeys, dma="xs")
                if L >= 0:
                    is_a = (L % 2 == 0)
                    j = L // 2
                    a_tile = attnT_d[:, t0:t0 + T].rearrange("(k p) t -> p k t", p=128)
                    cx.op("sp", lambda E: E.dma_start(out=act16, in_=a_tile), w=akeys, dma="a16")
                    Wo = (a_w_o if is_a else b_w_o)[j]
                    for og in range(D // 512):
                        fm_proj(Wo, og * 512, 512,
                                lambda oc, th, pi, og=og: resid_add(og * 4 + oc, th, pi))
                    norm(c.DEPTH + L, lambda ch, th: act16[:, ch, th * 512:(th + 1) * 512],
                         lambda ch, th: [f"a16_{th}"])
                    Win, Wout = mlp_w_in[L], mlp_w_out[L]
                    NG = c.DFF // 512
                    pend = None

                    def stage2(g, hi, wo, wok):
                        for ch in range(KC):
                            for th in range(TH):
                                pi = nxt("ps", NPS)
                                tsl = slice(th * 512, (th + 1) * 512)

                                def mm(E, ch=ch, tsl=tsl, pi=pi):
                                    out = []
                                    for k4 in range(4):
                                        out.append(E.matmul(pst[pi], wo[:, k4, ch * 128:(ch + 1) * 128],
                                                            hbuf[hi][:, k4, tsl], start=(k4 == 0),
                                                            stop=(k4 == 3), skip_group_check=True))
                                    return out
                                cx.op("pe", mm, r=[wok, f"h{hi}_{th}"], w=[f"ps{pi}"])
                                resid_add(ch, th, pi)

                    wi_next = load_w_slot(0, wcols(Win, 0, 512), KC, 512)
                    for g in range(NG):
                        hi = g % 2
                        wi_cur = wi_next
                        if g + 1 < NG:
                            wi_next = load_w_slot((g + 1) % 2, wcols(Win, (g + 1) * 512, 512), KC, 512)
                        wo, wok = load_w_slot(2 + g % 2,
                                              Wout[g * 512:(g + 1) * 512, :].rearrange("(k p) n -> p k n", p=128),
                                              4, D)

                        def ev1(oc, th, pi, hi=hi):
                            tsl = slice(th * 512, (th + 1) * 512)
                            ri = nxt("rt", 2)
                            cx.op("act", lambda E: E.activation(out=rtmp[ri], in_=pst[pi], func=AF.Relu),
                                  r=[f"ps{pi}"], w=[f"rt{ri}"])
                            cx.op("act", lambda E: E.activation(out=hbuf[hi][:, oc, tsl], in_=rtmp[ri],
                                                                func=AF.Square),
                                  r=[f"rt{ri}"], w=[f"h{hi}_{th}"])
                        fm_proj(None, None, 512, ev1, pre=wi_cur)
                        if pend is not None:
                            stage2(*pend)
                        pend = (g, hi, wo, wok)
                    stage2(*pend)

                if L < c.DEPTH - 1:
                    Ln = L + 1
                    is_a = (Ln % 2 == 0)
                    j = Ln // 2
                    norm(Ln, lambda ch, th: act16[:, ch, th * 512:(th + 1) * 512],
                         lambda ch, th: [f"a16_{th}"])
                    W = (a_w_qkv if is_a else b_w_qkv)[j]
                    dk = c.DKV if is_a else D

                    def qk_groups(col_base, ncols_total, dst, scale):
                        col = 0
                        while col < ncols_total:
                            ncol = min(512, ncols_total - col)

                            def ev(oc, th, pi, col=col):
                                tsl = slice(th * 512, (th + 1) * 512)
                                if th == 0:
                                    ev.si = nxt("stg", 4)
                                si = ev.si
                                cx.op("act", lambda E: E.activation(out=stg[si][:, tsl], in_=pst[pi],
                                                                    func=AF.Copy, scale=scale),
                                      r=[f"ps{pi}"], w=[f"stg{si}"])
                                if th == TH - 1:
                                    row0 = col + oc * 128
                                    cx.op("sp", lambda E: E.dma_start(
                                        out=dst[row0:row0 + 128, t0:t0 + T], in_=stg[si]),
                                        r=[f"stg{si}"], dma=f"stg{si}")
                            fm_proj(W, col_base + col, ncol, ev)
                            col += ncol

                    qk_groups(0, D, qT_d, 1.0 / math.sqrt(HD))
                    qk_groups(D, dk, kT_d, 1.0)
                    col = 0
                    while col < dk:
                        ncol = min(512, dk - col)
                        wv, wk = load_w(wcols(W, D + dk + col, ncol), KC, ncol)
                        for tb in range(T // 128):
                            pi = nxt("ps", NPS)
                            th = tb // 4

                            def mm(E, tb=tb, pi=pi, wv=wv, ncol=ncol):
                                out = []
                                for kc in range(KC):
                                    out.append(E.matmul(pst[pi][:, 0:ncol], act16[:, kc, tb * 128:(tb + 1) * 128],
                                                        wv[:, kc, :], start=(kc == 0), stop=(kc == KC - 1),
                                                        skip_group_check=True))
                                return out
                            cx.op("pe", mm, r=[wk, f"a16_{th}"], w=[f"ps{pi}"])
                            vi = nxt("vst", 2)
                            cx.op("dve", lambda E, pi=pi, vi=vi, ncol=ncol: E.tensor_copy(
                                out=vst[vi][:, 0:ncol], in_=pst[pi][:, 0:ncol]),
                                r=[f"ps{pi}"], w=[f"vst{vi}"])
                            cx.op("sp", lambda E, vi=vi, tb=tb, col=col, ncol=ncol: E.dma_start(
                                out=v_d[t0 + tb * 128:t0 + (tb + 1) * 128, col:col + ncol],
                                in_=vst[vi][:, 0:ncol]), r=[f"vst{vi}"], dma=f"vst{vi}")
                        col += ncol
                    if L >= 0:
                        cx.op("sp", lambda E: E.dma_start(out=xres_tile, in_=xs), r=xkeys, dma="xst")
                else:
                    def ydst(ch, th):
                        yi = nxt("yst", 2)
                        ydst.last = yi
                        return yst[yi]
                    norm(2 * c.DEPTH, ydst, lambda ch, th: [f"yst{ydst.last}"], final=True)
            cx.barrier()

    def att_b(j):
        with ExitStack() as es:
            def sb(name, shape, dt):
                return es.enter_context(nc.sbuf_tensor(f"{name}_b{j}", list(shape), dt))[:]

            def ps(name, shape, dt=F32):
                return es.enter_context(nc.psum_tensor(f"{name}_b{j}", list(shape), dt))[:]

            Kt = [sb(f"Kt{i}", [128, S], BF16) for i in range(2)]
            Qt = [sb(f"Qt{i}", [128, S], BF16) for i in range(2)]
            Vraw = [sb(f"Vr{i}", [128, NB, 128], BF16) for i in range(2)]
            Vpad = [sb(f"Vp{i}", [128, NB, 2, 128], BF16) for i in range(2)]
            NSL = 3
            e_sb = [sb(f"e{i}", [128, 2, 512], F32) for i in range(NSL)]
            sp_sb = [sb(f"sp{i}", [128, 2, 512], BF16) for i in range(NSL)]
            ec_sb = [sb(f"ec{i}", [128, 2, 512], F32) for i in range(2)]
            a_sb = [sb(f"a{i}", [128, 2, 512], BF16) for i in range(NSL)]
            ost = [sb(f"ost{i}", [128, 512], BF16) for i in range(2)]
            Z = [ps(f"Z{i}", [128, 2, 512]) for i in range(2)]
            P = ps("P", [128, 2, 512])
            O = [ps(f"O{i}", [128, 512]) for i in range(2)]

            for i in range(2):
                cx.op("pool", lambda E, i=i: E.memset(Vpad[i], 0.0), w=[f"Vp{i}"])

            steps = []
            for hp in range(c.NH // 2):
                for g in range(NB // 4):
                    for kb in range(4 * g + 3, -1, -1):
                        steps.append(dict(hp=hp, bi=hp % 2, g=g, oi=g % 2, kb=kb,
                                          first=(kb == 4 * g + 3), last=(kb == 0),
                                          new_hp=(g == 0 and kb == 3)))
            for n, st in enumerate(steps):
                st["zi"] = n % 2
                st["si"] = n % NSL

            def loads(hp):
                bi = hp % 2
                rows = slice(hp * 128, (hp + 1) * 128)
                cx.op("sp", lambda E: E.dma_start(out=Kt[bi], in_=kT_d[rows, :]), w=[f"Kt{bi}"], dma=f"Kt{bi}")
                cx.op("sp", lambda E: E.dma_start(out=Qt[bi], in_=qT_d[rows, :]), w=[f"Qt{bi}"], dma=f"Qt{bi}")
                cx.op("sp", lambda E: E.dma_start(
                    out=Vraw[bi], in_=v_d[:, rows].rearrange("(kb p) f -> p kb f", p=128)),
                    w=[f"Vr{bi}"], dma=f"Vr{bi}")
                cx.op("pool", lambda E: E.tensor_copy(out=Vpad[bi][:, :, 0, 0:64], in_=Vraw[bi][:, :, 0:64]),
                      r=[f"Vr{bi}"], w=[f"Vp{bi}"])
                cx.op("pool", lambda E: E.tensor_copy(out=Vpad[bi][:, :, 1, 64:128], in_=Vraw[bi][:, :, 64:128]),
                      r=[f"Vr{bi}"], w=[f"Vp{bi}"])

            def geom(st):
                g, kb = st["g"], st["kb"]
                diag = kb >= 4 * g
                c0 = (kb - 4 * g) * 128 if diag else 0
                return diag, c0, g * 512, slice(kb * 128, (kb + 1) * 128)

            def stage_a(st):
                diag, c0, q0, ksl = geom(st)
                bi, zi, si = st["bi"], st["zi"], st["si"]
                for hh in range(2):
                    prt = slice(hh * 64, (hh + 1) * 64)

                    def mmz(E, hh=hh, prt=prt):
                        out = [E.matmul(Z[zi][:, hh, c0:512], Kt[bi][prt, ksl],
                                        Qt[bi][prt, q0 + c0:q0 + 512], start=True, stop=not diag,
                                        skip_group_check=True)]
                        if diag:
                            out.append(E.matmul(Z[zi][:, hh, c0:c0 + 128], ident, maskdiag,
                                                start=False, stop=True, skip_group_check=True))
                        return out
                    cx.op("pe", mmz, r=[f"Kt{bi}", f"Qt{bi}", "cmat"], w=[f"Z{zi}_{hh}"])
                cx.op("act", lambda E: E.activation(
                    out=e_sb[si][:, :, c0:512], in_=Z[zi][:, :, c0:512], func=AF.Exp),
                    r=[f"Z{zi}_0", f"Z{zi}_1"], w=[f"e{si}"])
                cx.op("act", lambda E: E.activation(
                    out=sp_sb[si][:, :, c0:512], in_=e_sb[si][:, :, c0:512], func=AF.Ln,
                    bias=1.0, scale=1.0),
                    r=[f"e{si}"], w=[f"sp{si}"])

            def stage_b(st, prev):
                diag, c0, q0, ksl = geom(st)
                si, ei = st["si"], st["zi"]
                first = st["first"]
                for hh in range(2):
                    cx.op("pe", lambda E, hh=hh: E.matmul(
                        P[:, hh, c0:512], negtri, sp_sb[si][:, hh, c0:512],
                        start=first, stop=True, skip_group_check=True),
                        r=[f"sp{si}", "cmat"], w=[f"P{hh}"])
                cx.op("act", lambda E: E.activation(
                    out=ec_sb[ei][:, :, c0:512], in_=P[:, :, c0:512], func=AF.Exp),
                    r=["P0", "P1"], w=[f"ec{ei}"])
                cx.op("dve", lambda E: E.tensor_tensor(
                    out=a_sb[si][:, :, c0:512], in0=e_sb[si][:, :, c0:512],
                    in1=ec_sb[ei][:, :, c0:512], op=ALU.mult),
                    r=[f"e{si}", f"ec{ei}"], w=[f"a{si}"])
                if prev is not None:
                    stage_o(prev)
                for hh in range(2):
                    cx.op("pe", lambda E, hh=hh: E.matmul(
                        P[:, hh, c0:512], negsup, sp_sb[si][:, hh, c0:512],
                        start=False, stop=True, skip_group_check=True),
                        r=[f"sp{si}", "cmat"], w=[f"P{hh}"])

            def stage_o(st):
                diag, c0, q0, ksl = geom(st)
                bi, oi, si, kb = st["bi"], st["oi"], st["si"], st["kb"]
                for hh in range(2):
                    cx.op("pe", lambda E, hh=hh: E.matmul(
                        O[oi][:, c0:512], Vpad[bi][:, kb, hh, :], a_sb[si][:, hh, c0:512],
                        start=(st["first"] and hh == 0), stop=True, skip_group_check=True),
                        r=[f"a{si}", f"Vp{bi}"], w=[f"O{oi}"])
                if st["last"]:
                    rows = slice(st["hp"] * 128, (st["hp"] + 1) * 128)
                    cx.op("dve", lambda E: E.tensor_copy(out=ost[oi], in_=O[oi]), r=[f"O{oi}"], w=[f"ost{oi}"])
                    cx.op("sp", lambda E: E.dma_start(out=attnT_d[rows, q0:q0 + 512], in_=ost[oi]),
                          r=[f"ost{oi}"], dma=f"ost{oi}")

            loads(0)
            ns = len(steps)
            pending_load = None
            stage_a(steps[0])
            for n in range(ns):
                if pending_load is not None:
                    loads(pending_load)
                    pending_load = None
                if n + 1 < ns:
                    stage_a(steps[n + 1])
                stage_b(steps[n], steps[n - 1] if n >= 1 else None)
                if n >= 1 and steps[n - 1]["new_hp"] and steps[n - 1]["hp"] + 1 < c.NH // 2:
                    pending_load = steps[n - 1]["hp"] + 1
            stage_o(steps[ns - 1])
            cx.barrier()


    def att_a(j):
        with ExitStack() as es:
            def sb(name, shape, dt):
                return es.enter_context(nc.sbuf_tensor(f"{name}_a{j}", list(shape), dt))[:]

            def ps(name, shape, dt=F32):
                return es.enter_context(nc.psum_tensor(f"{name}_a{j}", list(shape), dt))[:]

            biasA = sb("biasA", [128, c.NH, 256], F32)
            Kt = [sb(f"Kt{i}", [128, S], BF16) for i in range(2)]
            Qt = [sb(f"Qt{i}", [128, S], BF16) for i in range(2)]
            Vraw = [sb(f"Vr{i}", [128, NB, 64], BF16) for i in range(2)]
            Vpad = [sb(f"Vp{i}", [128, NB, 2, 128], BF16) for i in range(2)]
            orow = [sb(f"orow{i}", [128, S], BF16) for i in range(2)]
            s_sb = [sb(f"s{i}", [128, 8, 256], F32) for i in range(2)]
            p_sb = [sb(f"p{i}", [128, 8, 256], F32) for i in range(2)]
            pn_sb = [sb(f"pn{i}", [128, 8, 256], BF16) for i in range(2)]
            pT_sb = [sb(f"pT{i}", [128, 8, 256], BF16) for i in range(2)]
            sm = [sb(f"sm{i}", [128, 6, 8], F32) for i in range(2)]
            Sps = [ps(f"S{i}", [128, 2, 256]) for i in range(4)]
            PTps = [ps(f"PT{i}", [128, 4, 256], BF16) for i in range(2)]
            Ops = [ps(f"O{i}", [128, 512]) for i in range(2)]

            cx.op("sp", lambda E: E.dma_start(out=biasA, in_=biasA_d.rearrange("p (h k) -> p h k", k=256)),
                  w=["biasA"], dma="biasA")
            for i in range(2):
                cx.op("pool", lambda E, i=i: E.memset(Vpad[i], 0.0), w=[f"Vp{i}"])
            batches = [[0]] + [list(range(b0, min(b0 + 4, NB))) for b0 in range(1, NB, 4)]
            nbatch = 0
            cidx = 0
            for kvh in range(c.NKV):
                ki = kvh % 2
                krow = slice(kvh * 64, (kvh + 1) * 64)
                cx.op("sp", lambda E: [E.dma_start(out=Kt[ki][0:64, :], in_=kT_d[krow, :]),
                                       E.dma_start(out=Kt[ki][64:128, :], in_=kT_d[krow, :])],
                      w=[f"Kt{ki}"], dma=f"Kt{ki}")
                cx.op("sp", lambda E: E.dma_start(
                    out=Vraw[ki], in_=v_d[:, krow].rearrange("(kb p) f -> p kb f", p=128)),
                    w=[f"Vr{ki}"], dma=f"Vr{ki}")
                cx.op("pool", lambda E: E.tensor_copy(out=Vpad[ki][:, :, 0, 0:64], in_=Vraw[ki]),
                      r=[f"Vr{ki}"], w=[f"Vp{ki}"])
                cx.op("pool", lambda E: E.tensor_copy(out=Vpad[ki][:, :, 1, 64:128], in_=Vraw[ki]),
                      r=[f"Vr{ki}"], w=[f"Vp{ki}"])
                for cc in range(4):
                    ch = kvh * 4 + cc
                    qi = cidx % 2
                    cidx += 1
                    rows = slice(ch * 128, (ch + 1) * 128)
                    cx.op("sp", lambda E: E.dma_start(out=Qt[qi], in_=qT_d[rows, :]), w=[f"Qt{qi}"], dma=f"Qt{qi}")
                    for blks in batches:
                        bp = nbatch % 2
                        nbatch += 1
                        nq = len(blks)
                        nkb = 1 if blks[0] == 0 else 2
                        N = nkb * 128
                        b0 = 256 - N
                        units = [(hh, qq) for hh in range(2) for qq in range(nq)]
                        nu = len(units)

                        def uidx(hh, qq):
                            return hh * nq + qq
                        for hh, qq in units:
                            u = uidx(hh, qq)
                            jb = blks[qq]
                            prt = slice(hh * 64, (hh + 1) * 64)
                            k0 = (jb - nkb + 1) * 128
                            sbk = hh * 2 + qq // 2
                            cx.op("pe", lambda E, sbk=sbk, qq=qq, jb=jb, prt=prt, k0=k0: E.matmul(
                                Sps[sbk][:, qq % 2, 0:N], Qt[qi][prt, jb * 128:(jb + 1) * 128],
                                Kt[ki][prt, k0:k0 + N], start=True, stop=True, skip_group_check=True),
                                r=[f"Qt{qi}", f"Kt{ki}"], w=[f"S{sbk}"])
                        for hh, qq in units:
                            u = uidx(hh, qq)
                            h = 2 * ch + hh
                            sbk = hh * 2 + qq // 2
                            cx.op("dve", lambda E, u=u, h=h, sbk=sbk, qq=qq: E.tensor_tensor(
                                out=s_sb[bp][:, u, 0:N], in0=Sps[sbk][:, qq % 2, 0:N],
                                in1=biasA[:, h, b0:256], op=ALU.add),
                                r=[f"S{sbk}", "biasA"], w=[f"s{bp}_{u}"])
                        cx.op("dve", lambda E: E.tensor_reduce(out=sm[bp][:, 0, 0:nu], in_=s_sb[bp][:, 0:nu, 0:N],
                                                               axis=AX.X, op=ALU.max),
                              r=[f"s{bp}_{u}" for u in range(nu)], w=[f"sm{bp}_0"])
                        for hh in range(2):
                            h = 2 * ch + hh
                            sk = sinks[:, j * c.NH + h: j * c.NH + h + 1]
                            cx.op("dve", lambda E, hh=hh, sk=sk: E.tensor_scalar(
                                out=sm[bp][:, 1, hh * nq:(hh + 1) * nq], in0=sm[bp][:, 0, hh * nq:(hh + 1) * nq],
                                scalar1=sk, scalar2=-1.0, op0=ALU.max, op1=ALU.mult),
                                r=[f"sm{bp}_0", "sinks"], w=[f"sm{bp}_1"])
                        for hh, qq in units:
                            u = uidx(hh, qq)
                            cx.op("act", lambda E, u=u: E.activation(
                                out=p_sb[bp][:, u, 0:N], in_=s_sb[bp][:, u, 0:N], func=AF.Exp,
                                bias=sm[bp][:, 1, u:u + 1], scale=1.0),
                                r=[f"s{bp}_{u}", f"sm{bp}_1"], w=[f"p{bp}_{u}"])
                        for hh in range(2):
                            h = 2 * ch + hh
                            sk = sinks[:, j * c.NH + h: j * c.NH + h + 1]
                            cx.op("act", lambda E, hh=hh, sk=sk: E.activation(
                                out=sm[bp][:, 2, hh * nq:(hh + 1) * nq], in_=sm[bp][:, 1, hh * nq:(hh + 1) * nq],
                                func=AF.Exp, bias=sk, scale=1.0),
                                r=[f"sm{bp}_1", "sinks"], w=[f"sm{bp}_2"])
                        cx.op("dve", lambda E: E.tensor_reduce(out=sm[bp][:, 3, 0:nu], in_=p_sb[bp][:, 0:nu, 0:N],
                                                               axis=AX.X, op=ALU.add),
                              r=[f"p{bp}_{u}" for u in range(nu)], w=[f"sm{bp}_3"])
                        cx.op("dve", lambda E: E.tensor_tensor(out=sm[bp][:, 4, 0:nu], in0=sm[bp][:, 3, 0:nu],
                                                               in1=sm[bp][:, 2, 0:nu], op=ALU.add),
                              r=[f"sm{bp}_3", f"sm{bp}_2"], w=[f"sm{bp}_4"])
                        cx.op("dve", lambda E: E.reciprocal(out=sm[bp][:, 5, 0:nu], in_=sm[bp][:, 4, 0:nu]),
                              r=[f"sm{bp}_4"], w=[f"sm{bp}_5"])
                        for hh, qq in units:
                            u = uidx(hh, qq)
                            cx.op("dve", lambda E, u=u: E.tensor_scalar(
                                out=pn_sb[bp][:, u, 0:N], in0=p_sb[bp][:, u, 0:N],
                                scalar1=sm[bp][:, 5, u:u + 1], scalar2=None, op0=ALU.mult),
                                r=[f"p{bp}_{u}", f"sm{bp}_5"], w=[f"pn{bp}_{u}"])
                        for half in range((nu + 3) // 4):
                            us = list(range(half * 4, min(half * 4 + 4, nu)))

                            def tr(E, us=us, half=half):
                                out = []
                                for u in us:
                                    for b in range(nkb):
                                        out.append(E.transpose(PTps[half][:, u % 4, b * 128:(b + 1) * 128],
                                                               pn_sb[bp][:, u, b * 128:(b + 1) * 128], ident))
                                return out
                            cx.op("pe", tr, r=[f"pn{bp}_{u}" for u in us] + ["cmat"], w=[f"PT{half}"])
                            cx.op("act", lambda E, us=us, half=half: E.activation(
                                out=pT_sb[bp][:, us[0]:us[-1] + 1, 0:N], in_=PTps[half][:, 0:len(us), 0:N],
                                func=AF.Copy),
                                r=[f"PT{half}"], w=[f"pT{bp}_{half}"])
                        oi = bp

                        def mmo(E):
                            out = []
                            firstmm = True
                            for hh, qq in units:
                                u = uidx(hh, qq)
                                jb = blks[qq]
                                for b in range(nkb):
                                    kbi = jb - nkb + 1 + b
                                    out.append(E.matmul(Ops[oi][:, qq * 128:(qq + 1) * 128],
                                                        Vpad[ki][:, kbi, hh, :],
                                                        pT_sb[bp][:, u, b * 128:(b + 1) * 128],
                                                        start=firstmm, stop=True, skip_group_check=True))
                                    firstmm = False
                            return out
                        cx.op("pe", mmo, r=[f"pT{bp}_{hf}" for hf in range((nu + 3) // 4)] + [f"Vp{ki}"], w=[f"O{oi}"])
                        cx.op("dve", lambda E: E.tensor_copy(
                            out=orow[qi][:, blks[0] * 128:(blks[-1] + 1) * 128], in_=Ops[oi][:, 0:nq * 128]),
                            r=[f"O{oi}"], w=[f"orow{qi}"])
                    cx.op("sp", lambda E: E.dma_start(out=attnT_d[rows, :], in_=orow[qi]),
                          r=[f"orow{qi}"], dma=f"orow{qi}")
            cx.barrier()


    tok_phase(-1)
    for L in range(c.DEPTH):
        if L % 2 == 0:
            att_a(L // 2)
        else:
            att_b(L // 2)
        tok_phase(L)
    return nc, cx


def host_consts(cfg):
    c = cfg
    i = np.arange(128)
    ident = np.eye(128, dtype=np.float32)
    negtri = -(i[:, None] >= i[None, :]).astype(np.float32)
    negsup = -(i[:, None] < i[None, :]).astype(np.float32)
    maskdiag = np.where(i[:, None] < i[None, :], 0.0, NEG).astype(np.float32)
    cmat = np.concatenate([ident, negtri, negsup, maskdiag], axis=1).astype(ml_dtypes.bfloat16)
    q = np.arange(128)[:, None]
    k = np.arange(256)[None, :]
    dist = (q + 128) - k
    valid = (dist >= 0) & (dist < 128)
    slopes = np.power(2.0, -8.0 * (np.arange(c.NH, dtype=np.float32) + 1.0) / c.NH).astype(np.float32)
    bias = -slopes[None, :, None] * dist[:, None, :].astype(np.float32)
    bias = np.where(valid[:, None, :], bias, NEG).astype(np.float32)
    return cmat, np.ascontiguousarray(bias.reshape(128, c.NH * 256))


def fm(v, KC):
    return np.ascontiguousarray(v.reshape(KC, 128).T)


_CACHE = {}
CORE_OF_BATCH = [0, 2, 4, 6]


def run(cfg, inputs, n_cores=8, core_of_batch=None):
    c = cfg
    x = np.asarray(inputs["x"], dtype=np.float32)
    B = x.shape[0]
    if core_of_batch is None:
        core_of_batch = CORE_OF_BATCH[:B]
    key = (c.D, c.S, c.DEPTH, c.T)
    if key not in _CACHE:
        _CACHE[key] = build(c)[0]
    nc = _CACHE[key]
    cmat, biasA = host_consts(c)
    gains = np.concatenate(
        [fm(np.asarray(inputs["norm_mix"][i], np.float32), c.KC) for i in range(c.DEPTH)]
        + [fm(np.asarray(inputs["norm_mlp"][i], np.float32), c.KC) for i in range(c.DEPTH)]
        + [fm(np.asarray(inputs["final_norm"], np.float32), c.KC)], axis=1)
    sinks = np.ascontiguousarray(np.broadcast_to(
        np.asarray(inputs["a_sinks"], np.float32).reshape(1, -1), (128, c.NA * c.NH)))
    shared = {
        "a_w_qkv": np.ascontiguousarray(inputs["a_w_qkv"], dtype=np.float32),
        "a_w_o": np.ascontiguousarray(inputs["a_w_o"], dtype=np.float32),
        "b_w_qkv": np.ascontiguousarray(inputs["b_w_qkv"], dtype=np.float32),
        "b_w_o": np.ascontiguousarray(inputs["b_w_o"], dtype=np.float32),
        "mlp_w_in": np.ascontiguousarray(inputs["mlp_w_in"], dtype=np.float32),
        "mlp_w_out": np.ascontiguousarray(inputs["mlp_w_out"], dtype=np.float32),
        "gains": np.ascontiguousarray(gains), "sinks": sinks, "cmat": cmat, "biasA": biasA,
    }
    zero_x = np.zeros((c.D, c.S), np.float32)
    in_maps = []
    for core in range(n_cores):
        m = dict(shared)
        if core in core_of_batch:
            b = core_of_batch.index(core)
            m["xT"] = np.ascontiguousarray(x[b].T)
        else:
            m["xT"] = zero_x
        in_maps.append(m)
    res = run_bass_kernel_spmd(nc, in_maps, core_ids=list(range(n_cores)))
    out = np.empty((B, c.S, c.D), np.float32)
    for b, core in enumerate(core_of_batch):
        out[b] = np.asarray(res.results[core]["yT"], dtype=np.float32).T
    return out


def kernel(x, a_w_qkv, a_w_o, a_sinks, b_w_qkv, b_w_o, norm_mix, norm_mlp,
           mlp_w_in, mlp_w_out, final_norm):
    cfg = Cfg(D=2048, S=4096, DEPTH=4, T=1024)
    return run(cfg, dict(x=x, a_w_qkv=a_w_qkv, a_w_o=a_w_o, a_sinks=a_sinks, b_w_qkv=b_w_qkv,
                         b_w_o=b_w_o, norm_mix=norm_mix, norm_mlp=norm_mlp, mlp_w_in=mlp_w_in,
                         mlp_w_out=mlp_w_out, final_norm=final_norm))
```

```python
import math
from contextlib import ExitStack

import numpy as np
import ml_dtypes

import concourse.bass as bass
import concourse.mybir as mybir
from concourse.bass_utils import run_bass_kernel_spmd

F32 = mybir.dt.float32
BF16 = mybir.dt.bfloat16
AF = mybir.ActivationFunctionType
ALU = mybir.AluOpType
AX = mybir.AxisListType

HD = 64
BLK = 128
EPS = 1e-5
NEG = -30000.0
SEM_ROT = 16000


class Cfg:
    def __init__(self, D=2048, S=4096, DEPTH=4, T=1024):
        self.D, self.S, self.DEPTH, self.T = D, S, DEPTH, T
        self.KC = D // 128
        self.NH = D // HD
        self.NKV = self.NH // 8
        self.DKV = self.NKV * HD
        self.DFF = 4 * D
        self.NB = S // BLK
        self.NT = S // T
        self.TH = T // 512
        self.NA = (DEPTH + 1) // 2
        self.NBL = DEPTH // 2
        self.QKVA = D + 2 * self.DKV
        self.QKVB = 3 * D
        self.WS = 4 * D


class Ctx:
    def __init__(self, nc):
        self.nc = nc
        self.eng = {"pe": nc.tensor, "act": nc.scalar, "dve": nc.vector,
                    "pool": nc.gpsimd, "sp": nc.sync}
        self.sems = []
        self.cur = {}
        self.pe_sems = set()
        self.waited = {e: {} for e in self.eng}
        self.last_w = {}
        self.readers = {}
        self.dma = {}
        self.nops = 0

    def _newsem(self):
        self.sems.append(self.nc.alloc_semaphore(f"s{len(self.sems)}"))
        return len(self.sems) - 1

    def _tok(self, e):
        c = self.cur.get(e)
        if c is None or c[1] >= SEM_ROT:
            c = [self._newsem(), 0]
            self.cur[e] = c
            if e == "pe":
                self.pe_sems.add(c[0])
        c[1] += 1
        return (c[0], c[1])

    def op(self, e, fn, r=(), w=(), dma=None):
        deps = {}

        def add(tok):
            if tok is not None and deps.get(tok[0], 0) < tok[1]:
                deps[tok[0]] = tok[1]

        for k in r:
            add(self.last_w.get(k))
        for k in w:
            add(self.last_w.get(k))
            for tok in self.readers.get(k, {}).values():
                add(tok)
        d = None
        if dma is not None:
            d = self.dma.get(dma)
            if d is None:
                d = self.dma[dma] = [self._newsem(), 0]
            if d[1] > 0:
                add((d[0], d[1]))
        E = self.eng[e]
        for s, v in deps.items():
            if e == "pe" and s in self.pe_sems:
                continue
            if self.waited[e].get(s, 0) >= v:
                continue
            E.wait_ge(self.sems[s], v)
            self.waited[e][s] = v
        insts = fn(E)
        if not isinstance(insts, (list, tuple)):
            insts = [insts]
        self.nops += len(insts)
        if dma is not None:
            for i in insts:
                i.then_inc(self.sems[d[0]], 16)
                d[1] += 16
            tok = (d[0], d[1])
            rkey = ("dma", dma)
        else:
            tok = self._tok(e)
            insts[-1].then_inc(self.sems[tok[0]], 1)
            rkey = e
        for k in r:
            self.readers.setdefault(k, {})[rkey] = tok
        for k in w:
            self.last_w[k] = tok
            self.readers[k] = {}
        return tok

    def all_tokens(self):
        toks = [(c[0], c[1]) for c in self.cur.values()]
        toks += [(d[0], d[1]) for d in self.dma.values() if d[1] > 0]
        return toks

    def barrier(self, engines=None):
        toks = self.all_tokens()
        for e, E in self.eng.items():
            if engines is not None and e not in engines:
                continue
            for s, v in toks:
                if self.waited[e].get(s, 0) < v:
                    E.wait_ge(self.sems[s], v)
                    self.waited[e][s] = v
        self.last_w.clear()
        self.readers.clear()


def build(cfg):
    c = cfg
    D, S, KC, T, TH, NB = c.D, c.S, c.KC, c.T, c.TH, c.NB
    nc = bass.Bass("TRN2", target_bir_lowering=False)
    cx = Ctx(nc)

    def din(name, shape, dt=F32):
        return nc.dram_tensor(name, list(shape), dt, kind="ExternalInput").ap()

    xT = din("xT", [D, S])
    a_w_qkv = din("a_w_qkv", [c.NA, D, c.QKVA])
    a_w_o = din("a_w_o", [c.NA, D, D])
    b_w_qkv = din("b_w_qkv", [max(c.NBL, 1), D, c.QKVB])
    b_w_o = din("b_w_o", [max(c.NBL, 1), D, D])
    mlp_w_in = din("mlp_w_in", [c.DEPTH, D, c.DFF])
    mlp_w_out = din("mlp_w_out", [c.DEPTH, c.DFF, D])
    gains_d = din("gains", [128, (2 * c.DEPTH + 1) * KC])
    sinks_d = din("sinks", [128, c.NA * c.NH])
    cmat_d = din("cmat", [128, 5 * 128], BF16)
    biasA_d = din("biasA", [128, c.NH * 256])
    yT = nc.dram_tensor("yT", [D, S], F32, kind="ExternalOutput").ap()

    xres = nc.dram_tensor("xres", [D, S], F32).ap()
    qT_d = nc.dram_tensor("qT_d", [D, S], BF16).ap()
    kT_d = nc.dram_tensor("kT_d", [D, S], BF16).ap()
    v_d = nc.dram_tensor("v_d", [S, D], BF16).ap()
    attnT_d = nc.dram_tensor("attnT_d", [D, S], BF16).ap()

    gains = nc.alloc_sbuf_tensor("gains_sb", [128, (2 * c.DEPTH + 1) * KC], F32).ap()
    sinks = nc.alloc_sbuf_tensor("sinks_sb", [128, c.NA * c.NH], F32).ap()
    cmat = nc.alloc_sbuf_tensor("cmat_sb", [128, 640], BF16).ap()
    ones_bf = nc.alloc_sbuf_tensor("ones_bf", [128, 128], BF16).ap()
    ident = cmat[:, 0:128]
    negtri = cmat[:, 128:256]
    negsup = cmat[:, 256:384]
    maskdiag = cmat[:, 384:512]
    negones = cmat[:, 512:640]

    cx.op("sp", lambda E: E.dma_start(out=gains, in_=gains_d), w=["gains"], dma="c0")
    cx.op("sp", lambda E: E.dma_start(out=sinks, in_=sinks_d), w=["sinks"], dma="c1")
    cx.op("sp", lambda E: E.dma_start(out=cmat, in_=cmat_d), w=["cmat"], dma="c2")
    cx.op("pool", lambda E: E.memset(ones_bf, 1.0), w=["ones"])
    cx.barrier()

    def gain_col(idx, ch):
        return gains[:, idx * KC + ch: idx * KC + ch + 1]

    def tok_phase(L):
        with ExitStack() as es:
            def sb(name, shape, dt):
                return es.enter_context(nc.sbuf_tensor(f"{name}_t{L}", list(shape), dt))[:]

            def ps(name, shape, dt=F32):
                return es.enter_context(nc.psum_tensor(f"{name}_t{L}", list(shape), dt))[:]

            xs = sb("xs", [128, KC, T], F32)
            act16 = sb("act16", [128, KC, T], BF16)
            NW = 4
            wsl = [sb(f"w{i}", [128, c.WS], BF16) for i in range(NW)]
            hbuf = [sb(f"h{i}", [128, 4, T], BF16) for i in range(2)]
            rtmp = [sb(f"rt{i}", [128, 512], F32) for i in range(2)]
            sqt = [sb(f"sq{i}", [128, 512], BF16) for i in range(2)]
            rstd = [sb(f"rs{i}", [128, 512], F32) for i in range(2)]
            stg = [sb(f"stg{i}", [128, T], BF16) for i in range(4)]
            vst = [sb(f"vst{i}", [128, 512], BF16) for i in range(2)]
            yst = [sb(f"yst{i}", [128, 512], F32) for i in range(2)]
            NPS = 6
            pst = [ps(f"ps{i}", [128, 512]) for i in range(NPS)]
            pss = [ps(f"pss{i}", [128, 512]) for i in range(2)]
            cnt = {"w": 0, "ps": 0, "rt": 0, "sq": 0, "stg": 0, "vst": 0, "yst": 0, "pss": 0}

            def nxt(k, n):
                i = cnt[k] % n
                cnt[k] += 1
                return i

            def load_w(src3, nk, ncol):
                i = nxt("w", NW)
                view = wsl[i][:, 0:nk * ncol].rearrange("p (k n) -> p k n", n=ncol)
                cx.op("pool", lambda E: E.dma_start(out=view, in_=src3), w=[f"w{i}"], dma=f"w{i}")
                return view, f"w{i}"

            def load_w_slot(i, src3, nk, ncol):
                view = wsl[i][:, 0:nk * ncol].rearrange("p (k n) -> p k n", n=ncol)
                cx.op("pool", lambda E: E.dma_start(out=view, in_=src3), w=[f"w{i}"], dma=f"w{i}")
                return view, f"w{i}"

            def wcols(W2, col0, ncol):
                return W2[:, col0:col0 + ncol].rearrange("(k p) n -> p k n", p=128)

            def norm(gidx, dst_fn, dst_keys_fn, final=False):
                for th in range(TH):
                    tsl = slice(th * 512, (th + 1) * 512)
                    pi = nxt("pss", 2)
                    for ch in range(KC):
                        si = nxt("sq", 2)
                        cx.op("act", lambda E, ch=ch, si=si: E.activation(
                            out=sqt[si], in_=xs[:, ch, tsl], func=AF.Square),
                            r=[f"xs{ch}_{th}"], w=[f"sq{si}"])
                        cx.op("pe", lambda E, ch=ch, si=si: E.matmul(
                            pss[pi], ones_bf, sqt[si], start=(ch == 0), stop=(ch == KC - 1),
                            skip_group_check=True),
                            r=[f"sq{si}", "ones"], w=[f"pss{pi}"])
                    ri = th % 2
                    cx.op("act", lambda E: E.activation(
                        out=rstd[ri], in_=pss[pi], func=AF.Sqrt, scale=1.0 / D, bias=EPS),
                        r=[f"pss{pi}"], w=[f"rs{ri}"])
                    cx.op("dve", lambda E: E.reciprocal(out=rstd[ri], in_=rstd[ri]),
                          r=[f"rs{ri}"], w=[f"rs{ri}"])
                    for ch in range(KC):
                        dst = dst_fn(ch, th)
                        cx.op("dve", lambda E, ch=ch, dst=dst: E.scalar_tensor_tensor(
                            out=dst, in0=xs[:, ch, tsl], scalar=gain_col(gidx, ch), in1=rstd[ri],
                            op0=ALU.mult, op1=ALU.mult),
                            r=[f"xs{ch}_{th}", f"rs{ri}", "gains"], w=dst_keys_fn(ch, th))
                        if final:
                            yi = dst_keys_fn(ch, th)[0]
                            cx.op("sp", lambda E, ch=ch, dst=dst: E.dma_start(
                                out=yT_tile[:, ch, tsl], in_=dst), r=[yi], dma=yi)

            def fm_proj(W2, col0, ncol, evac, pre=None):
                wv, wk = pre if pre is not None else load_w(wcols(W2, col0, ncol), KC, ncol)
                for oc in range(ncol // 128):
                    for th in range(TH):
                        pi = nxt("ps", NPS)
                        tsl = slice(th * 512, (th + 1) * 512)

                        def mm(E, oc=oc, tsl=tsl, pi=pi):
                            out = []
                            for kc in range(KC):
                                out.append(E.matmul(pst[pi], wv[:, kc, oc * 128:(oc + 1) * 128],
                                                    act16[:, kc, tsl], start=(kc == 0),
                                                    stop=(kc == KC - 1), skip_group_check=True))
                            return out
                        cx.op("pe", mm, r=[wk, f"a16_{th}"], w=[f"ps{pi}"])
                        evac(oc, th, pi)

            def resid_add(ch, th, pi):
                tsl = slice(th * 512, (th + 1) * 512)
                cx.op("dve", lambda E: E.tensor_tensor(out=xs[:, ch, tsl], in0=xs[:, ch, tsl],
                                                       in1=pst[pi], op=ALU.add),
                      r=[f"ps{pi}", f"xs{ch}_{th}"], w=[f"xs{ch}_{th}"])

            for tt in range(c.NT):
                t0 = tt * T
                src = xT if L <= 0 else xres
                x_tile = src[:, t0:t0 + T].rearrange("(k p) t -> p k t", p=128)
                xres_tile = xres[:, t0:t0 + T].rearrange("(k p) t -> p k t", p=128)
                yT_tile = yT[:, t0:t0 + T].rearrange("(k p) t -> p k t", p=128)
                xkeys = [f"xs{ch}_{th}" for ch in range(KC) for th in range(TH)]
                akeys = [f"a16_{th}" for th in range(TH)]
                cx.op("sp", lambda E: E.dma_start(out=xs, in_=x_tile), w=xkeys, dma="xs")
                if L >= 0:
                    is_a = (L % 2 == 0)
                    j = L // 2
                    a_tile = attnT_d[:, t0:t0 + T].rearrange("(k p) t -> p k t", p=128)
                    cx.op("sp", lambda E: E.dma_start(out=act16, in_=a_tile), w=akeys, dma="a16")
                    Wo = (a_w_o if is_a else b_w_o)[j]
                    for og in range(D // 512):
                        fm_proj(Wo, og * 512, 512,
                                lambda oc, th, pi, og=og: resid_add(og * 4 + oc, th, pi))
                    norm(c.DEPTH + L, lambda ch, th: act16[:, ch, th * 512:(th + 1) * 512],
                         lambda ch, th: [f"a16_{th}"])
                    Win, Wout = mlp_w_in[L], mlp_w_out[L]
                    NG = c.DFF // 512
                    pend = None

                    def stage2(g, hi, wo, wok):
                        for ch in range(KC):
                            for th in range(TH):
                                pi = nxt("ps", NPS)
                                tsl = slice(th * 512, (th + 1) * 512)

                                def mm(E, ch=ch, tsl=tsl, pi=pi):
                                    out = []
                                    for k4 in range(4):
                                        out.append(E.matmul(pst[pi], wo[:, k4, ch * 128:(ch + 1) * 128],
                                                            hbuf[hi][:, k4, tsl], start=(k4 == 0),
                                                            stop=(k4 == 3), skip_group_check=True))
                                    return out
                                cx.op("pe", mm, r=[wok, f"h{hi}_{th}"], w=[f"ps{pi}"])
                                resid_add(ch, th, pi)

                    wi_next = load_w_slot(0, wcols(Win, 0, 512), KC, 512)
                    for g in range(NG):
                        hi = g % 2
                        wi_cur = wi_next
                        if g + 1 < NG:
                            wi_next = load_w_slot((g + 1) % 2, wcols(Win, (g + 1) * 512, 512), KC, 512)
                        wo, wok = load_w_slot(2 + g % 2,
                                              Wout[g * 512:(g + 1) * 512, :].rearrange("(k p) n -> p k n", p=128),
                                              4, D)

                        def ev1(oc, th, pi, hi=hi):
                            tsl = slice(th * 512, (th + 1) * 512)
                            ri = nxt("rt", 2)
                            cx.op("act", lambda E: E.activation(out=rtmp[ri], in_=pst[pi], func=AF.Relu),
                                  r=[f"ps{pi}"], w=[f"rt{ri}"])
                            cx.op("act", lambda E: E.activation(out=hbuf[hi][:, oc, tsl], in_=rtmp[ri],
                                                                func=AF.Square),
                                  r=[f"rt{ri}"], w=[f"h{hi}_{th}"])
                        fm_proj(None, None, 512, ev1, pre=wi_cur)
                        if pend is not None:
                            stage2(*pend)
                        pend = (g, hi, wo, wok)
                    stage2(*pend)

                if L < c.DEPTH - 1:
                    Ln = L + 1
                    is_a = (Ln % 2 == 0)
                    j = Ln // 2
                    norm(Ln, lambda ch, th: act16[:, ch, th * 512:(th + 1) * 512],
                         lambda ch, th: [f"a16_{th}"])
                    W = (a_w_qkv if is_a else b_w_qkv)[j]
                    dk = c.DKV if is_a else D

                    def qk_groups(col_base, ncols_total, dst, scale):
                        col = 0
                        while col < ncols_total:
                            ncol = min(512, ncols_total - col)

                            def ev(oc, th, pi, col=col):
                                tsl = slice(th * 512, (th + 1) * 512)
                                if th == 0:
                                    ev.si = nxt("stg", 4)
                                si = ev.si
                                cx.op("act", lambda E: E.activation(out=stg[si][:, tsl], in_=pst[pi],
                                                                    func=AF.Copy, scale=scale),
                                      r=[f"ps{pi}"], w=[f"stg{si}"])
                                if th == TH - 1:
                                    row0 = col + oc * 128
                                    cx.op("sp", lambda E: E.dma_start(
                                        out=dst[row0:row0 + 128, t0:t0 + T], in_=stg[si]),
                                        r=[f"stg{si}"], dma=f"stg{si}")
                            fm_proj(W, col_base + col, ncol, ev)
                            col += ncol

                    qk_groups(0, D, qT_d, 1.0 / math.sqrt(HD))
                    qk_groups(D, dk, kT_d, 1.0)
                    col = 0
                    while col < dk:
                        ncol = min(512, dk - col)
                        wv, wk = load_w(wcols(W, D + dk + col, ncol), KC, ncol)
                        for tb in range(T // 128):
                            pi = nxt("ps", NPS)
                            th = tb // 4

                            def mm(E, tb=tb, pi=pi, wv=wv, ncol=ncol):
                                out = []
                                for kc in range(KC):
                                    out.append(E.matmul(pst[pi][:, 0:ncol], act16[:, kc, tb * 128:(tb + 1) * 128],
                                                        wv[:, kc, :], start=(kc == 0), stop=(kc == KC - 1),
                                                        skip_group_check=True))
                                return out
                            cx.op("pe", mm, r=[wk, f"a16_{th}"], w=[f"ps{pi}"])
                            vi = nxt("vst", 2)
                            cx.op("dve", lambda E, pi=pi, vi=vi, ncol=ncol: E.tensor_copy(
                                out=vst[vi][:, 0:ncol], in_=pst[pi][:, 0:ncol]),
                                r=[f"ps{pi}"], w=[f"vst{vi}"])
                            cx.op("sp", lambda E, vi=vi, tb=tb, col=col, ncol=ncol: E.dma_start(
                                out=v_d[t0 + tb * 128:t0 + (tb + 1) * 128, col:col + ncol],
                                in_=vst[vi][:, 0:ncol]), r=[f"vst{vi}"], dma=f"vst{vi}")
                        col += ncol
                    if L >= 0:
                        cx.op("sp", lambda E: E.dma_start(out=xres_tile, in_=xs), r=xkeys, dma="xst")
                else:
                    def ydst(ch, th):
                        yi = nxt("yst", 2)
                        ydst.last = yi
                        return yst[yi]
                    norm(2 * c.DEPTH, ydst, lambda ch, th: [f"yst{ydst.last}"], final=True)
            cx.barrier()

    def att_b(j):
        with ExitStack() as es:
            def sb(name, shape, dt):
                return es.enter_context(nc.sbuf_tensor(f"{name}_b{j}", list(shape), dt))[:]

            def ps(name, shape, dt=F32):
                return es.enter_context(nc.psum_tensor(f"{name}_b{j}", list(shape), dt))[:]

            Kt = [sb(f"Kt{i}", [128, S], BF16) for i in range(2)]
            Qt = [sb(f"Qt{i}", [128, S], BF16) for i in range(2)]
            Vraw = [sb(f"Vr{i}", [128, NB, 128], BF16) for i in range(2)]
            Vpad = [sb(f"Vp{i}", [128, NB, 2, 128], BF16) for i in range(2)]
            NSL = 3
            e_sb = [sb(f"e{i}", [128, 2, 512], F32) for i in range(NSL)]
            sp_sb = [sb(f"sp{i}", [128, 2, 512], BF16) for i in range(NSL)]
            ec_sb = [sb(f"ec{i}", [128, 2, 512], F32) for i in range(2)]
            a_sb = [sb(f"a{i}", [128, 2, 512], BF16) for i in range(NSL)]
            ost = [sb(f"ost{i}", [128, 512], BF16) for i in range(2)]
            Z = ps("Z", [128, 2, 512])
            Pb = [ps(f"P{i}", [128, 2, 512]) for i in range(2)]
            O = [ps(f"O{i}", [128, 512]) for i in range(2)]

            for i in range(2):
                cx.op("pool", lambda E, i=i: E.memset(Vpad[i], 0.0), w=[f"Vp{i}"])

            steps = []
            for hp in range(c.NH // 2):
                for g in range(NB // 4):
                    for kb in range(4 * g + 3, -1, -1):
                        steps.append(dict(hp=hp, bi=hp % 2, g=g, oi=g % 2, kb=kb,
                                          first=(kb == 4 * g + 3), last=(kb == 0),
                                          new_hp=(g == 0 and kb == 3)))
            for n, st in enumerate(steps):
                st["zi"] = n % 2
                st["si"] = n % NSL
                st["cur"] = (4 * st["g"] + 3 - st["kb"]) % 2

            def loads(hp):
                bi = hp % 2
                rows = slice(hp * 128, (hp + 1) * 128)
                cx.op("sp", lambda E: E.dma_start(out=Kt[bi], in_=kT_d[rows, :]), w=[f"Kt{bi}"], dma=f"Kt{bi}")
                cx.op("sp", lambda E: E.dma_start(out=Qt[bi], in_=qT_d[rows, :]), w=[f"Qt{bi}"], dma=f"Qt{bi}")
                cx.op("sp", lambda E: E.dma_start(
                    out=Vraw[bi], in_=v_d[:, rows].rearrange("(kb p) f -> p kb f", p=128)),
                    w=[f"Vr{bi}"], dma=f"Vr{bi}")
                cx.op("pool", lambda E: E.tensor_copy(out=Vpad[bi][:, :, 0, 0:64], in_=Vraw[bi][:, :, 0:64]),
                      r=[f"Vr{bi}"], w=[f"Vp{bi}"])
                cx.op("pool", lambda E: E.tensor_copy(out=Vpad[bi][:, :, 1, 64:128], in_=Vraw[bi][:, :, 64:128]),
                      r=[f"Vr{bi}"], w=[f"Vp{bi}"])

            def geom(st):
                g, kb = st["g"], st["kb"]
                diag = kb >= 4 * g
                c0 = (kb - 4 * g) * 128 if diag else 0
                return diag, c0, g * 512, slice(kb * 128, (kb + 1) * 128)

            def stage_a(st):
                diag, c0, q0, ksl = geom(st)
                bi, si = st["bi"], st["si"]
                for hh in range(2):
                    prt = slice(hh * 64, (hh + 1) * 64)

                    def mmz(E, hh=hh, prt=prt):
                        out = [E.matmul(Z[:, hh, c0:512], Kt[bi][prt, ksl],
                                        Qt[bi][prt, q0 + c0:q0 + 512], start=True, stop=not diag,
                                        skip_group_check=True)]
                        if diag:
                            out.append(E.matmul(Z[:, hh, c0:c0 + 128], ident, maskdiag,
                                                start=False, stop=True, skip_group_check=True))
                        return out
                    cx.op("pe", mmz, r=[f"Kt{bi}", f"Qt{bi}", "cmat"], w=[f"Z_{hh}"])
                cx.op("act", lambda E: E.activation(
                    out=e_sb[si][:, :, c0:512], in_=Z[:, :, c0:512], func=AF.Exp),
                    r=["Z_0", "Z_1"], w=[f"e{si}"])
                cx.op("act", lambda E: E.activation(
                    out=sp_sb[si][:, :, c0:512], in_=e_sb[si][:, :, c0:512], func=AF.Ln,
                    bias=1.0, scale=1.0),
                    r=[f"e{si}"], w=[f"sp{si}"])

            def stage_b(st, prev):
                diag, c0, q0, ksl = geom(st)
                si, ei = st["si"], st["zi"]
                first = st["first"]
                cur, oth = st["cur"], 1 - st["cur"]
                for hh in range(2):
                    cx.op("pe", lambda E, hh=hh: E.matmul(
                        Pb[cur][:, hh, c0:512], negtri, sp_sb[si][:, hh, c0:512],
                        start=first, stop=True, skip_group_check=True),
                        r=[f"sp{si}", "cmat"], w=[f"P{cur}_{hh}"])
                if not st["last"]:
                    for hh in range(2):
                        cx.op("pe", lambda E, hh=hh: E.matmul(
                            Pb[oth][:, hh, c0:512], negones, sp_sb[si][:, hh, c0:512],
                            start=first, stop=True, skip_group_check=True),
                            r=[f"sp{si}", "cmat"], w=[f"P{oth}_{hh}"])
                cx.op("act", lambda E: E.activation(
                    out=ec_sb[ei][:, :, c0:512], in_=Pb[cur][:, :, c0:512], func=AF.Exp),
                    r=[f"P{cur}_0", f"P{cur}_1"], w=[f"ec{ei}"])
                cx.op("dve", lambda E: E.tensor_tensor(
                    out=a_sb[si][:, :, c0:512], in0=e_sb[si][:, :, c0:512],
                    in1=ec_sb[ei][:, :, c0:512], op=ALU.mult),
                    r=[f"e{si}", f"ec{ei}"], w=[f"a{si}"])
                if prev is not None:
                    stage_o(prev)

            def stage_o(st):
                diag, c0, q0, ksl = geom(st)
                bi, oi, si, kb = st["bi"], st["oi"], st["si"], st["kb"]
                cur = st["cur"]
                for hh in range(2):
                    cx.op("pe", lambda E, hh=hh: E.matmul(
                        O[oi][:, c0:512], Vpad[bi][:, kb, hh, :], a_sb[si][:, hh, c0:512],
                        start=(st["first"] and hh == 0), stop=True, skip_group_check=True),
                        r=[f"a{si}", f"Vp{bi}"], w=[f"O{oi}"])
                if not st["last"]:
                    for hh in range(2):
                        cx.op("pe", lambda E, hh=hh: E.matmul(
                            Pb[cur][:, hh, c0:512], negsup, sp_sb[si][:, hh, c0:512],
                            start=False, stop=True, skip_group_check=True),
                            r=[f"sp{si}", "cmat"], w=[f"P{cur}_{hh}"])
                else:
                    rows = slice(st["hp"] * 128, (st["hp"] + 1) * 128)
                    cx.op("dve", lambda E: E.tensor_copy(out=ost[oi], in_=O[oi]), r=[f"O{oi}"], w=[f"ost{oi}"])
                    cx.op("sp", lambda E: E.dma_start(out=attnT_d[rows, q0:q0 + 512], in_=ost[oi]),
                          r=[f"ost{oi}"], dma=f"ost{oi}")

            loads(0)
            ns = len(steps)
            pending_load = None
            stage_a(steps[0])
            for n in range(ns):
                if pending_load is not None:
                    loads(pending_load)
                    pending_load = None
                if n + 1 < ns:
                    stage_a(steps[n + 1])
                stage_b(steps[n], steps[n - 1] if n >= 1 else None)
                if n >= 1 and steps[n - 1]["new_hp"] and steps[n - 1]["hp"] + 1 < c.NH // 2:
                    pending_load = steps[n - 1]["hp"] + 1
            stage_o(steps[ns - 1])
            cx.barrier()


    def att_a(j):
        with ExitStack() as es:
            def sb(name, shape, dt):
                return es.enter_context(nc.sbuf_tensor(f"{name}_a{j}", list(shape), dt))[:]

            def ps(name, shape, dt=F32):
                return es.enter_context(nc.psum_tensor(f"{name}_a{j}", list(shape), dt))[:]

            biasA = sb("biasA", [128, c.NH, 256], F32)
            Kt = [sb(f"Kt{i}", [128, S], BF16) for i in range(2)]
            Qt = [sb(f"Qt{i}", [128, S], BF16) for i in range(2)]
            Vraw = [sb(f"Vr{i}", [128, NB, 64], BF16) for i in range(2)]
            Vpad = [sb(f"Vp{i}", [128, NB, 2, 128], BF16) for i in range(2)]
            orow = [sb(f"orow{i}", [128, S], BF16) for i in range(2)]
            s_sb = [sb(f"s{i}", [128, 8, 256], F32) for i in range(2)]
            p_sb = [sb(f"p{i}", [128, 8, 256], F32) for i in range(2)]
            pn_sb = [sb(f"pn{i}", [128, 8, 256], BF16) for i in range(2)]
            pT_sb = [sb(f"pT{i}", [128, 8, 256], BF16) for i in range(2)]
            sm = [sb(f"sm{i}", [128, 6, 8], F32) for i in range(2)]
            Sps = [ps(f"S{i}", [128, 2, 256]) for i in range(4)]
            PTps = [ps(f"PT{i}", [128, 4, 256], BF16) for i in range(2)]
            Ops = [ps(f"O{i}", [128, 512]) for i in range(2)]

            cx.op("sp", lambda E: E.dma_start(out=biasA, in_=biasA_d.rearrange("p (h k) -> p h k", k=256)),
                  w=["biasA"], dma="biasA")
            for i in range(2):
                cx.op("pool", lambda E, i=i: E.memset(Vpad[i], 0.0), w=[f"Vp{i}"])
            batches = [[0]] + [list(range(b0, min(b0 + 4, NB))) for b0 in range(1, NB, 4)]

            def mk_batch(bp, blks, ki, qi, ch, rows, last_in_chunk):
                nq = len(blks)
                nkb = 1 if blks[0] == 0 else 2
                N = nkb * 128
                b0 = 256 - N
                units = [(hh, qq) for hh in range(2) for qq in range(nq)]
                nu = len(units)
                oi = bp

                def uidx(hh, qq):
                    return hh * nq + qq

                def front():
                    for hh, qq in units:
                        jb = blks[qq]
                        prt = slice(hh * 64, (hh + 1) * 64)
                        k0 = (jb - nkb + 1) * 128
                        sbk = hh * 2 + qq // 2
                        cx.op("pe", lambda E, sbk=sbk, qq=qq, jb=jb, prt=prt, k0=k0: E.matmul(
                            Sps[sbk][:, qq % 2, 0:N], Qt[qi][prt, jb * 128:(jb + 1) * 128],
                            Kt[ki][prt, k0:k0 + N], start=True, stop=True, skip_group_check=True),
                            r=[f"Qt{qi}", f"Kt{ki}"], w=[f"S{sbk}"])
                    for hh, qq in units:
                        u = uidx(hh, qq)
                        h = 2 * ch + hh
                        sbk = hh * 2 + qq // 2
                        cx.op("dve", lambda E, u=u, h=h, sbk=sbk, qq=qq: E.tensor_tensor(
                            out=s_sb[bp][:, u, 0:N], in0=Sps[sbk][:, qq % 2, 0:N],
                            in1=biasA[:, h, b0:256], op=ALU.add),
                            r=[f"S{sbk}", "biasA"], w=[f"s{bp}_{u}"])
                    cx.op("dve", lambda E: E.tensor_reduce(out=sm[bp][:, 0, 0:nu], in_=s_sb[bp][:, 0:nu, 0:N],
                                                           axis=AX.X, op=ALU.max),
                          r=[f"s{bp}_{u}" for u in range(nu)], w=[f"sm{bp}_0"])
                    for hh in range(2):
                        h = 2 * ch + hh
                        sk = sinks[:, j * c.NH + h: j * c.NH + h + 1]
                        cx.op("dve", lambda E, hh=hh, sk=sk: E.tensor_scalar(
                            out=sm[bp][:, 1, hh * nq:(hh + 1) * nq], in0=sm[bp][:, 0, hh * nq:(hh + 1) * nq],
                            scalar1=sk, scalar2=-1.0, op0=ALU.max, op1=ALU.mult),
                            r=[f"sm{bp}_0", "sinks"], w=[f"sm{bp}_1"])
                    for hh, qq in units:
                        u = uidx(hh, qq)
                        cx.op("act", lambda E, u=u: E.activation(
                            out=p_sb[bp][:, u, 0:N], in_=s_sb[bp][:, u, 0:N], func=AF.Exp,
                            bias=sm[bp][:, 1, u:u + 1], scale=1.0),
                            r=[f"s{bp}_{u}", f"sm{bp}_1"], w=[f"p{bp}_{u}"])
                    for hh in range(2):
                        h = 2 * ch + hh
                        sk = sinks[:, j * c.NH + h: j * c.NH + h + 1]
                        cx.op("act", lambda E, hh=hh, sk=sk: E.activation(
                            out=sm[bp][:, 2, hh * nq:(hh + 1) * nq], in_=sm[bp][:, 1, hh * nq:(hh + 1) * nq],
                            func=AF.Exp, bias=sk, scale=1.0),
                            r=[f"sm{bp}_1", "sinks"], w=[f"sm{bp}_2"])

                def back():
                    cx.op("dve", lambda E: E.tensor_reduce(out=sm[bp][:, 3, 0:nu], in_=p_sb[bp][:, 0:nu, 0:N],
                                                           axis=AX.X, op=ALU.add),
                          r=[f"p{bp}_{u}" for u in range(nu)], w=[f"sm{bp}_3"])
                    cx.op("dve", lambda E: E.tensor_tensor(out=sm[bp][:, 4, 0:nu], in0=sm[bp][:, 3, 0:nu],
                                                           in1=sm[bp][:, 2, 0:nu], op=ALU.add),
                          r=[f"sm{bp}_3", f"sm{bp}_2"], w=[f"sm{bp}_4"])
                    cx.op("dve", lambda E: E.reciprocal(out=sm[bp][:, 5, 0:nu], in_=sm[bp][:, 4, 0:nu]),
                          r=[f"sm{bp}_4"], w=[f"sm{bp}_5"])
                    for hh, qq in units:
                        u = uidx(hh, qq)
                        cx.op("dve", lambda E, u=u: E.tensor_scalar(
                            out=pn_sb[bp][:, u, 0:N], in0=p_sb[bp][:, u, 0:N],
                            scalar1=sm[bp][:, 5, u:u + 1], scalar2=None, op0=ALU.mult),
                            r=[f"p{bp}_{u}", f"sm{bp}_5"], w=[f"pn{bp}_{u}"])
                    for half in range((nu + 3) // 4):
                        us = list(range(half * 4, min(half * 4 + 4, nu)))

                        def tr(E, us=us, half=half):
                            out = []
                            for u in us:
                                for b in range(nkb):
                                    out.append(E.transpose(PTps[half][:, u % 4, b * 128:(b + 1) * 128],
                                                           pn_sb[bp][:, u, b * 128:(b + 1) * 128], ident))
                            return out
                        cx.op("pe", tr, r=[f"pn{bp}_{u}" for u in us] + ["cmat"], w=[f"PT{half}"])
                        cx.op("act", lambda E, us=us, half=half: E.activation(
                            out=pT_sb[bp][:, us[0]:us[-1] + 1, 0:N], in_=PTps[half][:, 0:len(us), 0:N],
                            func=AF.Copy),
                            r=[f"PT{half}"], w=[f"pT{bp}_{half}"])

                    def mmo(E):
                        out = []
                        firstmm = True
                        for hh, qq in units:
                            u = uidx(hh, qq)
                            jb = blks[qq]
                            for b in range(nkb):
                                kbi = jb - nkb + 1 + b
                                out.append(E.matmul(Ops[oi][:, qq * 128:(qq + 1) * 128],
                                                    Vpad[ki][:, kbi, hh, :],
                                                    pT_sb[bp][:, u, b * 128:(b + 1) * 128],
                                                    start=firstmm, stop=True, skip_group_check=True))
                                firstmm = False
                        return out
                    cx.op("pe", mmo, r=[f"pT{bp}_{hf}" for hf in range((nu + 3) // 4)] + [f"Vp{ki}"], w=[f"O{oi}"])
                    cx.op("dve", lambda E: E.tensor_copy(
                        out=orow[qi][:, blks[0] * 128:(blks[-1] + 1) * 128], in_=Ops[oi][:, 0:nq * 128]),
                        r=[f"O{oi}"], w=[f"orow{qi}"])
                    if last_in_chunk:
                        cx.op("sp", lambda E: E.dma_start(out=attnT_d[rows, :], in_=orow[qi]),
                              r=[f"orow{qi}"], dma=f"orow{qi}")
                return front, back

            nbatch = 0
            cidx = 0
            pending = None
            for kvh in range(c.NKV):
                ki = kvh % 2
                krow = slice(kvh * 64, (kvh + 1) * 64)
                cx.op("sp", lambda E: [E.dma_start(out=Kt[ki][0:64, :], in_=kT_d[krow, :]),
                                       E.dma_start(out=Kt[ki][64:128, :], in_=kT_d[krow, :])],
                      w=[f"Kt{ki}"], dma=f"Kt{ki}")
                cx.op("sp", lambda E: E.dma_start(
                    out=Vraw[ki], in_=v_d[:, krow].rearrange("(kb p) f -> p kb f", p=128)),
                    w=[f"Vr{ki}"], dma=f"Vr{ki}")
                cx.op("pool", lambda E: E.tensor_copy(out=Vpad[ki][:, :, 0, 0:64], in_=Vraw[ki]),
                      r=[f"Vr{ki}"], w=[f"Vp{ki}"])
                cx.op("pool", lambda E: E.tensor_copy(out=Vpad[ki][:, :, 1, 64:128], in_=Vraw[ki]),
                      r=[f"Vr{ki}"], w=[f"Vp{ki}"])
                for cc in range(4):
                    ch = kvh * 4 + cc
                    qi = cidx % 2
                    cidx += 1
                    rows = slice(ch * 128, (ch + 1) * 128)
                    cx.op("sp", lambda E: E.dma_start(out=Qt[qi], in_=qT_d[rows, :]), w=[f"Qt{qi}"], dma=f"Qt{qi}")
                    for bidx, blks in enumerate(batches):
                        front, back = mk_batch(nbatch % 2, blks, ki, qi, ch, rows, bidx == len(batches) - 1)
                        nbatch += 1
                        front()
                        if pending is not None:
                            pending()
                        pending = back
            pending()
            cx.barrier()

    tok_phase(-1)
    for L in range(c.DEPTH):
        if L % 2 == 0:
            att_a(L // 2)
        else:
            att_b(L // 2)
        tok_phase(L)
    return nc, cx


def host_consts(cfg):
    c = cfg
    i = np.arange(128)
    ident = np.eye(128, dtype=np.float32)
    negtri = -(i[:, None] >= i[None, :]).astype(np.float32)
    negsup = -(i[:, None] < i[None, :]).astype(np.float32)
    maskdiag = np.where(i[:, None] < i[None, :], 0.0, NEG).astype(np.float32)
    negones = -np.ones((128, 128), np.float32)
    cmat = np.concatenate([ident, negtri, negsup, maskdiag, negones], axis=1).astype(ml_dtypes.bfloat16)
    q = np.arange(128)[:, None]
    k = np.arange(256)[None, :]
    dist = (q + 128) - k
    valid = (dist >= 0) & (dist < 128)
    slopes = np.power(2.0, -8.0 * (np.arange(c.NH, dtype=np.float32) + 1.0) / c.NH).astype(np.float32)
    bias = -slopes[None, :, None] * dist[:, None, :].astype(np.float32)
    bias = np.where(valid[:, None, :], bias, NEG).astype(np.float32)
    return cmat, np.ascontiguousarray(bias.reshape(128, c.NH * 256))


def fm(v, KC):
    return np.ascontiguousarray(v.reshape(KC, 128).T)


_CACHE = {}
CORE_OF_BATCH = [0, 2, 4, 6]


def run(cfg, inputs, n_cores=8, core_of_batch=None):
    c = cfg
    x = np.asarray(inputs["x"], dtype=np.float32)
    B = x.shape[0]
    if core_of_batch is None:
        core_of_batch = CORE_OF_BATCH[:B]
    key = (c.D, c.S, c.DEPTH, c.T)
    if key not in _CACHE:
        _CACHE[key] = build(c)[0]
    nc = _CACHE[key]
    cmat, biasA = host_consts(c)
    gains = np.concatenate(
        [fm(np.asarray(inputs["norm_mix"][i], np.float32), c.KC) for i in range(c.DEPTH)]
        + [fm(np.asarray(inputs["norm_mlp"][i], np.float32), c.KC) for i in range(c.DEPTH)]
        + [fm(np.asarray(inputs["final_norm"], np.float32), c.KC)], axis=1)
    sinks = np.ascontiguousarray(np.broadcast_to(
        np.asarray(inputs["a_sinks"], np.float32).reshape(1, -1), (128, c.NA * c.NH)))
    shared = {
        "a_w_qkv": np.ascontiguousarray(inputs["a_w_qkv"], dtype=np.float32),
        "a_w_o": np.ascontiguousarray(inputs["a_w_o"], dtype=np.float32),
        "b_w_qkv": np.ascontiguousarray(inputs["b_w_qkv"], dtype=np.float32),
        "b_w_o": np.ascontiguousarray(inputs["b_w_o"], dtype=np.float32),
        "mlp_w_in": np.ascontiguousarray(inputs["mlp_w_in"], dtype=np.float32),
        "mlp_w_out": np.ascontiguousarray(inputs["mlp_w_out"], dtype=np.float32),
        "gains": np.ascontiguousarray(gains), "sinks": sinks, "cmat": cmat, "biasA": biasA,
    }
    zero_x = np.zeros((c.D, c.S), np.float32)
    in_maps = []
    for core in range(n_cores):
        m = dict(shared)
        if core in core_of_batch:
            b = core_of_batch.index(core)
            m["xT"] = np.ascontiguousarray(x[b].T)
        else:
            m["xT"] = zero_x
        in_maps.append(m)
    res = run_bass_kernel_spmd(nc, in_maps, core_ids=list(range(n_cores)))
    out = np.empty((B, c.S, c.D), np.float32)
    for b, core in enumerate(core_of_batch):
        out[b] = np.asarray(res.results[core]["yT"], dtype=np.float32).T
    return out


def kernel(x, a_w_qkv, a_w_o, a_sinks, b_w_qkv, b_w_o, norm_mix, norm_mlp,
           mlp_w_in, mlp_w_out, final_norm):
    cfg = Cfg(D=2048, S=4096, DEPTH=4, T=1024)
    return run(cfg, dict(x=x, a_w_qkv=a_w_qkv, a_w_o=a_w_o, a_sinks=a_sinks, b_w_qkv=b_w_qkv,
                         b_w_o=b_w_o, norm_mix=norm_mix, norm_mlp=norm_mlp, mlp_w_in=mlp_w_in,
                         mlp_w_out=mlp_w_out, final_norm=final_norm))
```
